# Optimizing a Trainium2 kernel written in Bass

```python
import jax, jax.numpy as jnp
from jax import lax
import numpy as np

D_MODEL = 1024
BATCH = 8
SEQ = 8192
DEPTH = 2

CHUNK = 64
GMLP_BLOCK = 128
A_HEADS = 4
A_HEAD_DIM = 128
D_A = A_HEADS * A_HEAD_DIM
B_GROUPS = 4
D_B = 512
CONV_WIDTH = 3
D_FF = 2816
N_BRANCH = 2
D_IN = 2 * D_A + 3 * D_B + N_BRANCH * D_MODEL
RMS_EPS = 1e-6
LN_EPS = 1e-5

kernel_name = "hybrid_gmlp_shortconv_gated_encoder"


def rms_norm(x, g):
    xf = x.astype(jnp.float32)
    y = xf * lax.rsqrt(jnp.mean(xf * xf, axis=-1, keepdims=True) + RMS_EPS)
    return (y * g.astype(jnp.float32)).astype(x.dtype)


def layer_norm(x, g, b):
    xf = x.astype(jnp.float32)
    mu = jnp.mean(xf, axis=-1, keepdims=True)
    var = jnp.mean(jnp.square(xf - mu), axis=-1, keepdims=True)
    y = (xf - mu) * lax.rsqrt(var + LN_EPS)
    return (y * g.astype(jnp.float32) + b.astype(jnp.float32)).astype(x.dtype)


def causal_dwconv(x, w):
    k, c = w.shape
    return lax.conv_general_dilated(
        x, w[:, None, :].astype(x.dtype), window_strides=(1,), padding=[(k - 1, 0)],
        dimension_numbers=("NWC", "WIO", "NWC"), feature_group_count=c)


def spatial_mask():
    idx = jnp.arange(GMLP_BLOCK) // CHUNK
    return idx[None, :] <= idx[:, None]


def gmlp_branch(u, v, ln_g, ln_b, w_s, b_s, mask):
    bsz, s, _ = u.shape
    u = jax.nn.gelu(u)
    v = layer_norm(jax.nn.gelu(v), ln_g, ln_b)
    vb = v.reshape(bsz, s // GMLP_BLOCK, GMLP_BLOCK, A_HEADS, A_HEAD_DIM)
    w_m = jnp.where(mask[None], w_s, jnp.zeros((), w_s.dtype))
    f = jnp.einsum("hij,bnjhd->bnihd", w_m, vb) + b_s.T[None, None, :, :, None]
    return u * f.reshape(bsz, s, D_A)


def setup_inputs(seed: int = 0) -> dict:
    key = jax.random.key(seed)
    ks = jax.random.split(key, 20)
    n = jax.random.normal
    f32 = jnp.float32
    return {
        "x": n(ks[0], (BATCH, SEQ, D_MODEL), f32),
        "norm1_g": 1.0 + 0.02 * n(ks[1], (DEPTH, D_MODEL), f32),
        "w_in": n(ks[2], (DEPTH, D_MODEL, D_IN), f32) * D_MODEL ** -0.5,
        "b_gate": 0.02 * n(ks[3], (DEPTH, N_BRANCH * D_MODEL), f32),
        "gmlp_ln_g": 1.0 + 0.02 * n(ks[4], (DEPTH, D_A), f32),
        "gmlp_ln_b": 0.02 * n(ks[5], (DEPTH, D_A), f32),
        "w_spatial": n(ks[6], (DEPTH, A_HEADS, GMLP_BLOCK, GMLP_BLOCK), f32) * (0.5 * GMLP_BLOCK ** -0.5),
        "b_spatial": 1.0 + 0.02 * n(ks[7], (DEPTH, A_HEADS, GMLP_BLOCK), f32),
        "w_shortconv": n(ks[8], (DEPTH, CONV_WIDTH, D_B), f32) * CONV_WIDTH ** -0.5,
        "w_branch": n(ks[9], (DEPTH, N_BRANCH, D_A, D_MODEL), f32) * D_A ** -0.5,
        "w_out": n(ks[10], (DEPTH, D_MODEL, D_MODEL), f32) * D_MODEL ** -0.5,
        "norm2_g": 1.0 + 0.02 * n(ks[11], (DEPTH, D_MODEL), f32),
        "w_ffn_up": n(ks[12], (DEPTH, D_MODEL, 2 * D_FF), f32) * D_MODEL ** -0.5,
        "w_ffn_conv": n(ks[13], (DEPTH, CONV_WIDTH, D_FF), f32) * CONV_WIDTH ** -0.5,
        "b_ffn_conv": 0.02 * n(ks[14], (DEPTH, D_FF), f32),
        "w_ffn_down": n(ks[15], (DEPTH, D_FF, D_MODEL), f32) * D_FF ** -0.5,
        "final_g": 1.0 + 0.02 * n(ks[16], (D_MODEL,), f32),
    }


def reference(x, norm1_g, w_in, b_gate, gmlp_ln_g, gmlp_ln_b, w_spatial, b_spatial,
              w_shortconv, w_branch, w_out, norm2_g, w_ffn_up, w_ffn_conv, b_ffn_conv,
              w_ffn_down, final_g):
    mask = spatial_mask()
    cuts = [D_A, 2 * D_A, 2 * D_A + D_B, 2 * D_A + 2 * D_B, 2 * D_A + 3 * D_B,
            2 * D_A + 3 * D_B + D_MODEL]
    for l in range(DEPTH):
        h = rms_norm(x, norm1_g[l])
        z = h @ w_in[l]
        u, v, bg, cg, hb, ga, gb = jnp.split(z, cuts, axis=-1)
        y_a = gmlp_branch(u, v, gmlp_ln_g[l], gmlp_ln_b[l], w_spatial[l], b_spatial[l], mask)
        y_b = bg * causal_dwconv(cg * hb, w_shortconv[l])
        g_a = jax.nn.sigmoid(ga + b_gate[l, :D_MODEL])
        g_b = jax.nn.sigmoid(gb + b_gate[l, D_MODEL:])
        merged = g_a * (y_a @ w_branch[l, 0]) + g_b * (y_b @ w_branch[l, 1])
        x = x + merged @ w_out[l]
        h = rms_norm(x, norm2_g[l])
        up = h @ w_ffn_up[l]
        gate, val = up[..., :D_FF], up[..., D_FF:]
        gate = causal_dwconv(gate, w_ffn_conv[l]) + b_ffn_conv[l]
        x = x + (jax.nn.silu(gate) * val) @ w_ffn_down[l]
    return rms_norm(x, final_g)
```

```python
import contextlib
import numpy as np
import concourse.bass as bass
import concourse.mybir as mybir
from concourse.bass_utils import run_bass_kernel_spmd

F32 = mybir.dt.float32
BF16 = mybir.dt.bfloat16
AF = mybir.ActivationFunctionType
ALU = mybir.AluOpType

D = 1024
DA = 512
DB = 512
DFF = 2816
DIN = 4608
DEPTH = 2
T = 512
NCH = 8
NFF = 22
SLOT = 4096
NSLOT = 7
NSCR = 10
RMS_EPS = 1e-6
LN_EPS = 1e-5

LS = 136
O_N1G, O_BG, O_LNG, O_WSC, O_N2G, O_WFC, O_BFC = 0, 8, 24, 28, 40, 48, 114
O_FG = 2 * LS
NS = O_FG + 8


class Buf:
    __slots__ = ("name", "w", "rd", "rd_dma", "gen")

    def __init__(self, name):
        self.name = name
        self.w = None
        self.rd = {}
        self.rd_dma = []
        self.gen = 0


class Op:
    __slots__ = ("eng", "fn", "deps", "sig", "need", "dma")


class Prog:
    ENGS = ("pe", "act", "dve", "pool", "sp")

    def __init__(self):
        self.q = {e: [] for e in self.ENGS}
        self.dma_cnt = {}
        self.last_dma = {}
        self.final = []

    def add(self, eng, fn, rd=(), wr=(), dma=None):
        op = Op()
        op.eng = eng
        op.fn = fn
        op.need = False
        op.dma = dma
        op.sig = None
        deps = []
        for b in rd:
            if b.w is not None:
                deps.append(b.w)
        for b in wr:
            if b.w is not None:
                deps.append(b.w)
            deps.extend(b.rd.values())
            deps.extend(b.rd_dma)
        if dma is not None and dma in self.last_dma:
            deps.append(self.last_dma[dma])
        seen = set()
        dl = []
        for d in deps:
            if id(d) in seen:
                continue
            seen.add(id(d))
            if eng == "pe" and d.eng == "pe" and d.dma is None:
                continue
            dl.append(d)
            d.need = True
        op.deps = dl
        for b in wr:
            b.w = op
            b.rd = {}
            b.rd_dma = []
        for b in rd:
            if dma is not None:
                b.rd_dma.append(op)
            else:
                b.rd[eng] = op
        if dma is not None:
            self.dma_cnt[dma] = self.dma_cnt.get(dma, 0) + 16
            op.sig = (dma, self.dma_cnt[dma])
            op.need = True
            self.last_dma[dma] = op
        self.q[eng].append(op)
        return op

    def emit(self, block, sems):
        for e in ("pe", "act", "dve", "pool"):
            cnt = 0
            for op in self.q[e]:
                if op.dma is None and op.need:
                    cnt += 1
                    op.sig = (e, cnt)

        def run(e, eng):
            waited = {}
            for op in self.q[e]:
                for d in op.deps:
                    sname, val = d.sig
                    if waited.get(sname, 0) >= val:
                        continue
                    eng.wait_ge(sems[sname], val)
                    waited[sname] = val
                ins = op.fn(eng)
                if op.dma is not None:
                    ins.then_inc(sems[op.dma], 16)
                elif op.need:
                    ins.then_inc(sems[e], 1)
            for (fe, sname) in self.final:
                if fe == e:
                    eng.wait_ge(sems[sname], self.dma_cnt[sname])

        @block.tensor
        def _(t):
            run("pe", t)

        @block.scalar
        def _(s):
            run("act", s)

        @block.vector
        def _(v):
            run("dve", v)

        @block.gpsimd
        def _(g):
            run("pool", g)

        @block.sync
        def _(sp):
            run("sp", sp)


class Ring:
    def __init__(self, items):
        self.items = items
        self.i = 0

    def get(self):
        b, ap = self.items[self.i % len(self.items)]
        self.i += 1
        return b, ap


def piece_table():
    names = []
    for g in (0, 1, 2, 3, 4):
        names.append(("G%d" % g, 4096))
    for grp in (0, 1):
        names.append(("G%d" % (5 + grp), 4096))
        names.append(("G%d" % (7 + grp), 4096))
        names.append(("PA%d" % grp, 2048))
        names.append(("PB%d" % grp, 2048))
    names.append(("WO0", 4096))
    names.append(("WO1", 4096))
    for i in range(11):
        names.append(("UP%d" % i, 4096))
    for cp in range(4):
        names.append(("DN0_%d" % cp, 2816))
        names.append(("DN1_%d" % cp, 2816))
    tab = []
    off = 0
    for n, sz in names:
        tab.append({"name": n, "size": sz, "off": off})
        off += sz
    return tab, off


PIECES, WTOT = piece_table()
NPIECE = len(PIECES)


def build(S):
    NT = S // T
    nc = bass.Bass("TRN2", target_bir_lowering=False)
    P = Prog()

    x_d = nc.dram_tensor("x", [S, D], F32, kind="ExternalInput")
    o_d = nc.dram_tensor("out", [S, D], F32, kind="ExternalOutput")
    win_d = nc.dram_tensor("w_in", [DEPTH * D, DIN], F32, kind="ExternalInput")
    wbr_d = nc.dram_tensor("w_branch", [DEPTH * 2 * DA, D], F32, kind="ExternalInput")
    wout_d = nc.dram_tensor("w_out", [DEPTH * D, D], F32, kind="ExternalInput")
    wup_d = nc.dram_tensor("w_up", [DEPTH * D, 2 * DFF], F32, kind="ExternalInput")
    wdn_d = nc.dram_tensor("w_down", [DEPTH * DFF, D], F32, kind="ExternalInput")
    sm_d = nc.dram_tensor("smalls", [128, NS], F32, kind="ExternalInput")
    wmt_d = nc.dram_tensor("wmt", [DEPTH * 4 * 128, 128], F32, kind="ExternalInput")
    lnb_d = nc.dram_tensor("lnb", [1, DEPTH * DA], F32, kind="ExternalInput")
    bsr_d = nc.dram_tensor("bsr", [1, DEPTH * DA], F32, kind="ExternalInput")
    idn_d = nc.dram_tensor("ident", [128, 128], F32, kind="ExternalInput")
    ws_d = nc.dram_tensor("wscratch", [DEPTH * 128, WTOT], BF16)

    with contextlib.ExitStack() as es:
        def sb(name, shape, dt):
            return es.enter_context(nc.sbuf_tensor(name, shape, dt))

        xs = sb("xs", [128, NCH, T], F32)
        xin = sb("xin", [128, 4, D], F32)
        hs = sb("hs", [128, NCH, T], BF16)
        sqs = sb("sqs", [128, 4, T], BF16)
        guv = sb("guv", [128, 8, T], F32)
        vns = sb("vns", [128, 4, T], BF16)
        yas = sb("yas", [128, 4, T], BF16)
        ybs = sb("ybs", [128, 4, T], BF16)
        mgs = sb("mgs", [128, NCH, T], BF16)
        acts = sb("acts", [128, NFF, T], BF16)
        pbs = sb("pbs", [128, 2, T + 2], F32)
        gbs = sb("gbs", [128, 3, T + 2], F32)
        scr = sb("scr", [128, NSCR, T], F32)
        ring = sb("ring", [128, NSLOT * SLOT], BF16)
        ident = sb("identsb", [128, 128], F32)
        ones_b = sb("ones_b", [128, 128], BF16)
        ones_f = sb("ones_f", [1, 128], F32)
        wmt_b = sb("wmt_b", [128, DEPTH * 4, 128], BF16)
        cst = sb("cst", [128, DEPTH * 4, 128], F32)
        sm = sb("sm", [128, NS], F32)
        bsr = sb("bsrsb", [1, DEPTH * DA], F32)
        halo_b = sb("halo_b", [128, DEPTH * 4, 2], F32)
        halo_g = sb("halo_g", [128, DEPTH * NFF, 2], F32)
        stats = sb("stats", [128, 4, 6], F32)
        mv = sb("mv", [128, 4, 2], F32)
        sd4 = sb("sd4", [128, 4], F32)
        rs4 = sb("rs4", [128, 4], F32)
        nm4 = sb("nm4", [128, 4], F32)
        dm = sb("dm", [128, 2], F32)
        rstd_sb = sb("rstd_sb", [128, T], F32)
        rtm = sb("rtm", [128, 4], F32)
        psb = [es.enter_context(nc.psum_tensor("ps%d" % i, [128, T], F32)) for i in range(8)]

        NCS, NOS = 8, 4
        sem_names = ["pe", "act", "dve", "pool", "d_in", "d_misc"] + \
                    ["d_w%d" % i for i in range(NSLOT)] + ["d_c%d" % i for i in range(NCS)] + \
                    ["d_o%d" % i for i in range(NOS)]
        cnt = {"cast": 0, "out": 0}
        sems = {n: es.enter_context(nc.semaphore(n)) for n in sem_names}
        block = es.enter_context(nc.Block())

        B_x = [Buf("x%d" % c) for c in range(NCH)]
        B_xin = Buf("xin")
        B_h = [Buf("h%d" % c) for c in range(NCH)]
        B_guv = [Buf("guv%d" % c) for c in range(8)]
        B_vn = [Buf("vn%d" % c) for c in range(4)]
        B_ya = [Buf("ya%d" % c) for c in range(4)]
        B_yb = [Buf("yb%d" % c) for c in range(4)]
        B_mg = [Buf("mg%d" % c) for c in range(NCH)]
        B_act = [Buf("act%d" % c) for c in range(NFF)]
        B_const = Buf("const")
        B_halo_b = [Buf("hb%d" % i) for i in range(DEPTH * 4)]
        B_halo_g = [Buf("hg%d" % i) for i in range(DEPTH * NFF)]
        B_st = Buf("stats")
        B_slot = [Buf("slot%d" % i) for i in range(NSLOT)]
        B_ws = [[[Buf("ws%d_%d_%d" % (l, i, j)) for j in range(2)] for i in range(NPIECE)] for l in range(DEPTH)]

        R_ps = Ring([(Buf("ps%d" % i), psb[i]) for i in range(7)])
        B_ss = Buf("ss")
        ps_ss = psb[7]
        B_dm = Buf("dm")
        B_rstd = Buf("rstd")
        B_rtm = Buf("rtm")
        R_scr = Ring([(Buf("scr%d" % i), scr[:, i, :]) for i in range(NSCR)])
        R_sq = Ring([(Buf("sq%d" % i), sqs[:, i, :]) for i in range(4)])
        R_pb = Ring([((Buf("pb%d" % i), Buf("pbh%d" % i)), pbs[:, i, :]) for i in range(2)])
        R_gb = Ring([((Buf("gb%d" % i), Buf("gbh%d" % i)), gbs[:, i, :]) for i in range(3)])

        def smc(col):
            return sm[:, col:col + 1]

        P.add("sp", lambda e: e.dma_start(out=sm[:], in_=sm_d.ap()[:, :]), wr=[B_const], dma="d_misc")
        P.add("sp", lambda e: e.dma_start(out=ident[:], in_=idn_d.ap()[:, :]), wr=[B_const], dma="d_misc")
        P.add("sp", lambda e: e.dma_start(out=bsr[:], in_=bsr_d.ap()[:, :]), wr=[B_const], dma="d_misc")
        STG = [Buf("stg")] + B_guv[0:4]
        wm_f = guv[:, 0:2, :].rearrange("p c (a i) -> p (c a) i", a=4)
        lnb_bc = guv[:, 2:4, :].rearrange("p c t -> p (c t)")
        P.add("sp", lambda e: e.dma_start(
            out=wm_f, in_=wmt_d.ap().rearrange("(a j) i -> j a i", j=128)), wr=STG, dma="d_misc")
        P.add("sp", lambda e: e.dma_start(
            out=lnb_bc, in_=lnb_d.ap()[0:1, :].broadcast_to([128, DEPTH * DA])), wr=STG, dma="d_misc")
        P.add("pool", lambda e: e.memset(ones_b[:], 1.0), wr=[B_const])
        P.add("pool", lambda e: e.memset(ones_f[:], 1.0), wr=[B_const])
        P.add("pool", lambda e: e.memset(dm[:], 1.0), wr=[B_dm])
        P.add("pool", lambda e: e.memset(halo_b[:], 0.0), wr=B_halo_b)
        P.add("pool", lambda e: e.memset(halo_g[:], 0.0), wr=B_halo_g)
        P.add("dve", lambda e: e.memset(wm_f[64:128, :, 0:64], 0.0), rd=STG, wr=STG)
        P.add("dve", lambda e: e.tensor_copy(out=wmt_b[:], in_=wm_f), rd=STG, wr=[B_const])
        for l in range(DEPTH):
            pb_, pap = R_ps.get()
            for hd in range(4):
                a = l * 4 + hd
                P.add("pe", lambda e, a=a, hd=hd, l=l, pap=pap: e.matmul(
                    pap[:, hd * 128:(hd + 1) * 128],
                    lhsT=lnb_bc[:, l * DA + hd * 128: l * DA + (hd + 1) * 128],
                    rhs=wm_f[:, a, :], start=True, stop=False), rd=STG, wr=[pb_])
                P.add("pe", lambda e, a=a, hd=hd, l=l, pap=pap: e.matmul(
                    pap[:, hd * 128:(hd + 1) * 128],
                    lhsT=ones_f[0:1, :],
                    rhs=bsr[0:1, l * DA + hd * 128: l * DA + (hd + 1) * 128],
                    start=False, stop=True), rd=[B_const], wr=[pb_])
            P.add("act", lambda e, l=l, pap=pap: e.activation(
                out=cst[:, l * 4:(l + 1) * 4, :],
                in_=pap[:].rearrange("p (a i) -> p a i", a=4), func=AF.Copy), rd=[pb_], wr=[B_const])

        def load_x(n):
            t0 = n * T
            src = bass.AP(x_d, t0 * D, [[D, 128], [128 * D, 4], [1, D]])
            P.add("pool", lambda e: e.dma_start(out=xin[:], in_=src), wr=[B_xin], dma="d_in")

        load_x(0)

        def cast_piece(l, pi):
            pc = PIECES[pi]
            n = pc["name"]
            off = pc["off"]
            row0 = l * 128
            dsts_srcs = []
            if n[0] == "G":
                g = int(n[1:])
                src = bass.AP(win_d, l * D * DIN + g * 512, [[DIN, 128], [128 * DIN, 8], [1, 512]])
                dst = bass.AP(ws_d, row0 * WTOT + off, [[WTOT, 128], [512, 8], [1, 512]])
                dsts_srcs.append((dst, src))
            elif n[0] == "P":
                br = 0 if n[1] == "A" else 1
                g = int(n[2:])
                src = bass.AP(wbr_d, (l * 2 + br) * DA * D + g * 512, [[D, 128], [128 * D, 4], [1, 512]])
                dst = bass.AP(ws_d, row0 * WTOT + off, [[WTOT, 128], [512, 4], [1, 512]])
                dsts_srcs.append((dst, src))
            elif n[0] == "W":
                g = int(n[2:])
                src = bass.AP(wout_d, l * D * D + g * 512, [[D, 128], [128 * D, 8], [1, 512]])
                dst = bass.AP(ws_d, row0 * WTOT + off, [[WTOT, 128], [512, 8], [1, 512]])
                dsts_srcs.append((dst, src))
            elif n[0] == "U":
                i = int(n[2:])
                for gv in range(2):
                    src = bass.AP(wup_d, l * D * 2 * DFF + gv * DFF + i * 256,
                                  [[2 * DFF, 128], [128 * 2 * DFF, 8], [1, 256]])
                    dst = bass.AP(ws_d, row0 * WTOT + off + gv * 256, [[WTOT, 128], [512, 8], [1, 256]])
                    dsts_srcs.append((dst, src))
            else:
                kh = int(n[2])
                cp = int(n[4:])
                src = bass.AP(wdn_d, (l * DFF + kh * 11 * 128) * D + cp * 256,
                              [[D, 128], [128 * D, 11], [1, 256]])
                dst = bass.AP(ws_d, row0 * WTOT + off, [[WTOT, 128], [256, 11], [1, 256]])
                dsts_srcs.append((dst, src))
            for j, (dst, src) in enumerate(dsts_srcs):
                P.add("pool", lambda e, dst=dst, src=src: e.dma_start(out=dst, in_=src),
                      wr=[B_ws[l][pi][j]], dma="d_c%d" % (cnt["cast"] % NCS))
                cnt["cast"] += 1

        for l in range(DEPTH):
            for pi in range(NPIECE):
                cast_piece(l, pi)

        NQ = NT * DEPTH * NPIECE
        wst = {"next_load": 0, "next_acq": 0, "next_rel": 0}

        def issue_load():
            q = wst["next_load"]
            if q >= NQ:
                return
            wst["next_load"] += 1
            l = (q // NPIECE) % DEPTH
            pi = q % NPIECE
            pc = PIECES[pi]
            s = q % NSLOT
            src = bass.AP(ws_d, l * 128 * WTOT + pc["off"], [[WTOT, 128], [1, pc["size"]]])
            dst = ring[:, s * SLOT: s * SLOT + pc["size"]]
            P.add("sp", lambda e, dst=dst, src=src: e.dma_start(out=dst, in_=src),
                  rd=B_ws[l][pi], wr=[B_slot[s]], dma="d_w%d" % s)

        def acquire(name):
            q = wst["next_acq"]
            wst["next_acq"] += 1
            pc = PIECES[q % NPIECE]
            assert pc["name"] == name, (pc["name"], name)
            s = q % NSLOT
            return B_slot[s], s * SLOT

        def release(k=1):
            for _ in range(k):
                wst["next_rel"] += 1
                assert wst["next_rel"] <= wst["next_acq"]
                issue_load()

        for _ in range(NSLOT):
            issue_load()

        def mm(out_ap, lhsT, rhs, start, stop, rd, wr):
            P.add("pe", lambda e: e.matmul(out_ap, lhsT=lhsT, rhs=rhs, start=start, stop=stop), rd=rd, wr=wr)

        def act_fn(out_ap, in_ap, func, rd, wr, scale=None, bias=None):
            kw = {}
            if scale is not None:
                kw["scale"] = scale
            if bias is not None:
                kw["bias"] = bias
            P.add("act", lambda e: e.activation(out=out_ap, in_=in_ap, func=func, **kw), rd=rd, wr=wr)

        def stt(out_ap, in0, scalar, in1, op0, op1, rd, wr):
            P.add("dve", lambda e: e.scalar_tensor_tensor(out=out_ap, in0=in0, scalar=scalar, in1=in1,
                                                          op0=op0, op1=op1), rd=rd, wr=wr)

        def tt(out_ap, in0, in1, op, rd, wr, eng="dve"):
            P.add(eng, lambda e: e.tensor_tensor(out=out_ap, in0=in0, in1=in1, op=op), rd=rd, wr=wr)

        nst = {"pend": None, "n": 0}

        def sqrt_preload():
            act_fn(dm[:, 0:1], dm[:, 1:2], AF.Sqrt, rd=[B_dm], wr=[B_dm])

        def norm_flush():
            if nst["pend"] is not None:
                qb, qap = nst["pend"]
                i = nst["n"]
                mm(ps_ss[:], ones_b[:], qap, i == 0, i == NCH - 1, rd=[qb, B_const], wr=[B_ss])
                nst["n"] = i + 1
                nst["pend"] = None

        HP = [(B_vn[k], vns[:, k, :]) for k in range(4)] + [(B_ya[k], yas[:, k, :]) for k in range(4)]

        def norm_sq(c, gcol=None):
            qb, qap = R_sq.get()
            act_fn(qap, xs[:, c, :], AF.Square, rd=[B_x[c]], wr=[qb])
            if gcol is not None:
                hb_, hap = HP[c]
                act_fn(hap, xs[:, c, :], AF.Copy, rd=[B_x[c], B_const], wr=[hb_], scale=smc(gcol + c))
            norm_flush()
            nst["pend"] = (qb, qap)

        def norm_rstd():
            norm_flush()
            assert nst["n"] == NCH
            nst["n"] = 0
            sb_, sap = R_scr.get()
            act_fn(sap, ps_ss[:], AF.Sqrt, rd=[B_ss], wr=[sb_], scale=1.0 / D, bias=RMS_EPS)
            P.add("dve", lambda e: e.reciprocal(out=rstd_sb[:], in_=sap), rd=[sb_], wr=[B_rstd])

        def norm_apply(gcol, outs, cs):
            for c in cs:
                ob, oap = outs[c]
                stt(oap, xs[:, c, :], smc(gcol + c), rstd_sb[:], ALU.mult, ALU.mult,
                    rd=[B_x[c], B_rstd, B_const], wr=[ob])

        def fixup(pb_, pap):
            tt(pap[:], pap[:], rstd_sb[:], ALU.mult, rd=[pb_, B_rstd], wr=[pb_])

        def layer(l):
            so = l * LS
            H = [(B_h[c], hs[:, c, :]) for c in range(NCH)]
            norm_rstd()
            wb, wo = acquire("G0")
            for c in range(4):
                pb_, pap = R_ps.get()
                for k in range(NCH):
                    mm(pap[:], ring[:, wo + k * 512 + c * 128: wo + k * 512 + (c + 1) * 128], HP[k][1],
                       k == 0, k == NCH - 1, rd=[wb, HP[k][0]], wr=[pb_])
                fixup(pb_, pap)
                act_fn(guv[:, c, :], pap[:], AF.Gelu_apprx_tanh, rd=[pb_], wr=[B_guv[c]])
            release()
            pbt, papt = R_ps.get()
            for b in range(4):
                mm(papt[:, 2 * b:2 * b + 2], rstd_sb[0:1, b * 128:(b + 1) * 128], ones_f[0:1, 0:2], True, True,
                   rd=[B_rstd, B_const], wr=[pbt])
            act_fn(rtm[:], papt[:, 0:8].rearrange("p (b two) -> p b two", two=2)[:, :, 0], AF.Copy,
                   rd=[pbt], wr=[B_rtm])
            norm_apply(so + O_N1G, H, range(NCH))
            wb, wo = acquire("G1")
            for b in range(4):
                pb_, pap = R_ps.get()
                for k in range(NCH):
                    mm(pap[:], HP[k][1][:, b * 128:(b + 1) * 128], ring[:, wo + k * 512: wo + (k + 1) * 512],
                       k == 0, k == NCH - 1, rd=[wb, HP[k][0]], wr=[pb_])
                act_fn(guv[:, 4 + b, :], pap[:], AF.Gelu_apprx_tanh, rd=[pb_, B_rtm], wr=[B_guv[4 + b]],
                       scale=rtm[:, b:b + 1])
                P.add("dve", lambda e, b=b: e.bn_stats(out=stats[:, b, :], in_=guv[:, 4 + b, :]),
                      rd=[B_guv[4 + b]], wr=[B_st])
                P.add("dve", lambda e, b=b: e.bn_aggr(out=mv[:, b, :], in_=stats[:, b, :]),
                      rd=[B_st], wr=[B_st])
            release()
            def ln_finalize():
                act_fn(sd4[:], mv[:, :, 1], AF.Sqrt, rd=[B_st], wr=[B_st], bias=LN_EPS)
                P.add("dve", lambda e: e.reciprocal(out=rs4[:], in_=sd4[:]), rd=[B_st], wr=[B_st])
                stt(nm4[:], mv[:, :, 0], -1.0, rs4[:], ALU.mult, ALU.mult, rd=[B_st], wr=[B_st])
                for b in range(4):
                    act_fn(vns[:, b, :], guv[:, 4 + b, :], AF.Identity, rd=[B_guv[4 + b], B_st], wr=[B_vn[b]],
                           scale=rs4[:, b:b + 1], bias=nm4[:, b:b + 1])

            def spatial_mix():
                for hd in range(4):
                    pb_, pap = R_ps.get()
                    for b in range(4):
                        mm(pap[:, b * 128:(b + 1) * 128], vns[:, b, hd * 128:(hd + 1) * 128],
                           wmt_b[:, l * 4 + hd, :], True, True, rd=[B_vn[b], B_const], wr=[pb_])
                    p3 = pap[:].rearrange("p (b i) -> p b i", b=4)
                    cb = cst[:, l * 4 + hd, :].unsqueeze(1).broadcast_to([128, 4, 128])
                    stt(p3, p3, smc(so + O_LNG + hd), cb, ALU.mult, ALU.add, rd=[pb_, B_const], wr=[pb_])
                    tt(yas[:, hd, :], pap[:], guv[:, hd, :], ALU.mult, rd=[pb_, B_guv[hd]], wr=[B_ya[hd]])
            wbB, woB = acquire("G2")
            wbC, woC = acquire("G3")
            wbH, woH = acquire("G4")
            def conv_chunk(c):
                pbB, papB = R_ps.get()
                pbC, papC = R_ps.get()
                pbH, papH = R_ps.get()
                for (wb_, wo_, pb_, pap_) in ((wbC, woC, pbC, papC), (wbH, woH, pbH, papH), (wbB, woB, pbB, papB)):
                    for k in range(NCH):
                        mm(pap_[:], ring[:, wo_ + k * 512 + c * 128: wo_ + k * 512 + (c + 1) * 128], hs[:, k, :],
                           k == 0, k == NCH - 1, rd=[wb_, B_h[k]], wr=[pb_])
                csb, csap = R_scr.get()
                act_fn(csap, papC[:], AF.Copy, rd=[pbC], wr=[csb])
                (qb, qh), qap = R_pb.get()
                hb = B_halo_b[l * 4 + c]
                P.add("dve", lambda e, qap=qap, c=c: e.tensor_copy(out=qap[:, 0:2], in_=halo_b[:, l * 4 + c, :]),
                      rd=[hb], wr=[qh])
                tt(qap[:, 2:T + 2], csap, papH[:], ALU.mult, rd=[csb, pbH], wr=[qb])
                P.add("act", lambda e, qap=qap, c=c: e.activation(out=halo_b[:, l * 4 + c, :], in_=qap[:, T:T + 2],
                                                                  func=AF.Copy), rd=[qb], wr=[hb])
                ab, aap = R_scr.get()
                act_fn(aap, qap[:, 2:T + 2], AF.Copy, rd=[qb, B_const], wr=[ab], scale=smc(so + O_WSC + 8 + c))
                b1, b1ap = R_scr.get()
                stt(b1ap, qap[:, 1:T + 1], smc(so + O_WSC + 4 + c), aap, ALU.mult, ALU.add,
                    rd=[qb, qh, ab, B_const], wr=[b1])
                c1, c1ap = R_scr.get()
                stt(c1ap, qap[:, 0:T], smc(so + O_WSC + c), b1ap, ALU.mult, ALU.add,
                    rd=[qb, qh, b1, B_const], wr=[c1])
                tt(ybs[:, c, :], c1ap, papB[:], ALU.mult, rd=[c1, pbB], wr=[B_yb[c]])
            conv_chunk(0)
            ln_finalize()
            conv_chunk(1)
            spatial_mix()
            conv_chunk(2)
            conv_chunk(3)
            release(3)
            for grp in range(2):
                wga, oga = acquire("G%d" % (5 + grp))
                wgb, ogb = acquire("G%d" % (7 + grp))
                wpa, opa = acquire("PA%d" % grp)
                wpb, opb = acquire("PB%d" % grp)
                for cc in range(4):
                    c = grp * 4 + cc
                    pga, apga = R_ps.get()
                    pgb, apgb = R_ps.get()
                    ppa, appa = R_ps.get()
                    ppb, appb = R_ps.get()
                    for k in range(NCH):
                        mm(apga[:], ring[:, oga + k * 512 + cc * 128: oga + k * 512 + (cc + 1) * 128], hs[:, k, :],
                           k == 0, k == NCH - 1, rd=[wga, B_h[k]], wr=[pga])
                    for k in range(NCH):
                        mm(apgb[:], ring[:, ogb + k * 512 + cc * 128: ogb + k * 512 + (cc + 1) * 128], hs[:, k, :],
                           k == 0, k == NCH - 1, rd=[wgb, B_h[k]], wr=[pgb])
                    for k in range(4):
                        mm(appa[:], ring[:, opa + k * 512 + cc * 128: opa + k * 512 + (cc + 1) * 128], yas[:, k, :],
                           k == 0, k == 3, rd=[wpa, B_ya[k]], wr=[ppa])
                    for k in range(4):
                        mm(appb[:], ring[:, opb + k * 512 + cc * 128: opb + k * 512 + (cc + 1) * 128], ybs[:, k, :],
                           k == 0, k == 3, rd=[wpb, B_yb[k]], wr=[ppb])
                    sab, saap = R_scr.get()
                    act_fn(saap, apga[:], AF.Sigmoid, rd=[pga, B_const], wr=[sab], bias=smc(so + O_BG + c))
                    sbb, sbap = R_scr.get()
                    act_fn(sbap, apgb[:], AF.Sigmoid, rd=[pgb, B_const], wr=[sbb], bias=smc(so + O_BG + 8 + c))
                    t1b, t1ap = R_scr.get()
                    tt(t1ap, saap, appa[:], ALU.mult, rd=[sab, ppa], wr=[t1b])
                    t2b, t2ap = R_scr.get()
                    tt(t2ap, sbap, appb[:], ALU.mult, rd=[sbb, ppb], wr=[t2b])
                    tt(mgs[:, c, :], t1ap, t2ap, ALU.add, rd=[t1b, t2b], wr=[B_mg[c]])
                release(4)
            wb0, wo0 = acquire("WO0")
            wb1, wo1 = acquire("WO1")
            sqrt_preload()
            for c in range(NCH):
                wb_, wo_ = (wb0, wo0) if c < 4 else (wb1, wo1)
                cc = c % 4
                pb_, pap = R_ps.get()
                for k in range(NCH):
                    mm(pap[:], ring[:, wo_ + k * 512 + cc * 128: wo_ + k * 512 + (cc + 1) * 128], mgs[:, k, :],
                       k == 0, k == NCH - 1, rd=[wb_, B_mg[k]], wr=[pb_])
                tt(xs[:, c, :], xs[:, c, :], pap[:], ALU.add, rd=[B_x[c], pb_], wr=[B_x[c]])
                norm_sq(c, so + O_N2G)
            release(2)
            norm_rstd()
            st8 = {}

            def s1(c):
                i, j = c // 2, c % 2
                if j == 0:
                    st8["w"] = acquire("UP%d" % i)
                wb_, wo_ = st8["w"]
                pg, apg = R_ps.get()
                pv, apv = R_ps.get()
                src = HP if c < 4 else H
                for k in range(NCH):
                    mm(apg[:], ring[:, wo_ + k * 512 + j * 128: wo_ + k * 512 + (j + 1) * 128], src[k][1],
                       k == 0, k == NCH - 1, rd=[wb_, src[k][0]], wr=[pg])
                for k in range(NCH):
                    mm(apv[:], ring[:, wo_ + k * 512 + 256 + j * 128: wo_ + k * 512 + 256 + (j + 1) * 128],
                       src[k][1], k == 0, k == NCH - 1, rd=[wb_, src[k][0]], wr=[pv])
                if j == 1:
                    release()
                if c < 4:
                    fixup(pg, apg)
                    fixup(pv, apv)
                    norm_apply(so + O_N2G, H, (2 * c, 2 * c + 1))
                (gb_, gh_), gap = R_gb.get()
                hg = B_halo_g[l * NFF + c]
                P.add("dve", lambda e: e.tensor_copy(out=gap[:, 0:2], in_=halo_g[:, l * NFF + c, :]),
                      rd=[hg], wr=[gh_])
                act_fn(gap[:, 2:T + 2], apg[:], AF.Copy, rd=[pg], wr=[gb_])
                act_fn(halo_g[:, l * NFF + c, :], apg[:, T - 2:T], AF.Copy, rd=[pg, gh_], wr=[hg])
                ab, aap = R_scr.get()
                act_fn(aap, apg[:], AF.Identity, rd=[pg, B_const], wr=[ab],
                       scale=smc(so + O_WFC + 2 * NFF + c), bias=smc(so + O_BFC + c))
                st8[c] = (gb_, gh_, gap, ab, aap, pv, apv)

            def s2(c):
                gb_, gh_, gap, ab, aap, pv, apv = st8[c]
                b1, b1ap = R_scr.get()
                stt(b1ap, gap[:, 1:T + 1], smc(so + O_WFC + NFF + c), aap, ALU.mult, ALU.add,
                    rd=[gb_, gh_, ab, B_const], wr=[b1])
                c1, c1ap = R_scr.get()
                stt(c1ap, gap[:, 0:T], smc(so + O_WFC + c), b1ap, ALU.mult, ALU.add,
                    rd=[gb_, gh_, b1, B_const], wr=[c1])
                sgb, sgap = R_scr.get()
                act_fn(sgap, c1ap, AF.Silu, rd=[c1], wr=[sgb])
                st8[c] = (sgb, sgap, pv, apv)

            def s3(c):
                sgb, sgap, pv, apv = st8[c]
                tt(acts[:, c, :], sgap, apv[:], ALU.mult, rd=[sgb, pv], wr=[B_act[c]])
                del st8[c]

            for it in range(NFF + 2):
                if it < NFF:
                    s1(it)
                if 0 <= it - 1 < NFF:
                    s2(it - 1)
                if 0 <= it - 2 < NFF:
                    s3(it - 2)
            sqrt_preload()
            for cp in range(4):
                wa, oa = acquire("DN0_%d" % cp)
                wb2, ob2 = acquire("DN1_%d" % cp)
                for j in range(2):
                    c = cp * 2 + j
                    pb_, pap = R_ps.get()
                    for k in range(NFF):
                        kh, kk = k // 11, k % 11
                        wb_, wo_ = (wa, oa) if kh == 0 else (wb2, ob2)
                        mm(pap[:], ring[:, wo_ + kk * 256 + j * 128: wo_ + kk * 256 + (j + 1) * 128], acts[:, k, :],
                           k == 0, k == NFF - 1, rd=[wb_, B_act[k]], wr=[pb_])
                    tt(xs[:, c, :], xs[:, c, :], pap[:], ALU.add, rd=[B_x[c], pb_], wr=[B_x[c]])
                    norm_sq(c, (l + 1) * LS + O_N1G if l + 1 < DEPTH else None)
                release(2)

        for n in range(NT):
            sqrt_preload()
            for c in range(NCH):
                pb_, pap = R_ps.get()
                for blk in range(4):
                    P.add("pe", lambda e, pap=pap, blk=blk, c=c: e.transpose(
                        out=pap[:, blk * 128:(blk + 1) * 128], in_=xin[:, blk, c * 128:(c + 1) * 128],
                        identity=ident[:]), rd=[B_xin, B_const], wr=[pb_])
                if c % 2 == 0:
                    act_fn(xs[:, c, :], pap[:], AF.Copy, rd=[pb_], wr=[B_x[c]])
                else:
                    P.add("dve", lambda e, pap=pap, c=c: e.tensor_copy(out=xs[:, c, :], in_=pap[:]),
                          rd=[pb_], wr=[B_x[c]])
                norm_sq(c, O_N1G)
            if n + 1 < NT:
                load_x(n + 1)
            for l in range(DEPTH):
                layer(l)
            norm_rstd()
            norm_apply(O_FG, [(B_guv[c], guv[:, c, :]) for c in range(NCH)], range(NCH))
            for blk in range(4):
                for half in range(2):
                    pb_, pap = R_ps.get()
                    for cc in range(4):
                        c = half * 4 + cc
                        P.add("pe", lambda e, pap=pap, blk=blk, c=c, cc=cc: e.transpose(
                            out=pap[:, cc * 128:(cc + 1) * 128], in_=guv[:, c, blk * 128:(blk + 1) * 128],
                            identity=ident[:]), rd=[B_guv[c], B_const], wr=[pb_])
                    ob, oap = R_scr.get()
                    act_fn(oap, pap[:], AF.Copy, rd=[pb_], wr=[ob])
                    dst = bass.AP(o_d, (n * T + blk * 128) * D + half * 512, [[D, 128], [1, 512]])
                    P.add("pool", lambda e, dst=dst, oap=oap: e.dma_start(out=dst, in_=oap),
                          rd=[ob], dma="d_o%d" % (cnt["out"] % NOS))
                    cnt["out"] += 1
        assert wst["next_acq"] == NQ and wst["next_rel"] == NQ, wst
        for i in range(NOS):
            P.final.append(("pool", "d_o%d" % i))
        P.emit(block, sems)
    return nc


def _cols(v):
    return np.ascontiguousarray(v.reshape(-1, 128).T)


def _taps(w):
    k, m = w.shape
    n = m // 128
    return np.ascontiguousarray(w.reshape(k, n, 128).transpose(2, 0, 1).reshape(128, k * n))


def _smalls(inp):
    parts = []
    for l in range(DEPTH):
        parts += [_cols(inp["norm1_g"][l]), _cols(inp["b_gate"][l]), _cols(inp["gmlp_ln_g"][l]),
                  _taps(inp["w_shortconv"][l]), _cols(inp["norm2_g"][l]), _taps(inp["w_ffn_conv"][l]),
                  _cols(inp["b_ffn_conv"][l])]
    parts.append(_cols(inp["final_g"]))
    sm = np.ascontiguousarray(np.concatenate(parts, axis=1).astype(np.float32))
    assert sm.shape == (128, NS), sm.shape
    return sm


_NC_CACHE = {}


def kernel(**inputs):
    inp = {k: np.asarray(v) for k, v in inputs.items()}
    x = inp["x"]
    Bsz, S, _ = x.shape
    assert Bsz == 8 and S % T == 0
    if S not in _NC_CACHE:
        _NC_CACHE[S] = build(S)
    nc = _NC_CACHE[S]
    shared = {
        "w_in": np.ascontiguousarray(inp["w_in"].reshape(DEPTH * D, DIN)),
        "w_branch": np.ascontiguousarray(inp["w_branch"].reshape(DEPTH * 2 * DA, D)),
        "w_out": np.ascontiguousarray(inp["w_out"].reshape(DEPTH * D, D)),
        "w_up": np.ascontiguousarray(inp["w_ffn_up"].reshape(DEPTH * D, 2 * DFF)),
        "w_down": np.ascontiguousarray(inp["w_ffn_down"].reshape(DEPTH * DFF, D)),
        "smalls": _smalls(inp),
        "wmt": np.ascontiguousarray(inp["w_spatial"].transpose(0, 1, 3, 2).reshape(DEPTH * 4 * 128, 128)),
        "lnb": np.ascontiguousarray(inp["gmlp_ln_b"].reshape(1, DEPTH * DA)),
        "bsr": np.ascontiguousarray(inp["b_spatial"].reshape(1, DEPTH * DA)),
        "ident": np.eye(128, dtype=np.float32),
    }
    in_maps = []
    for b in range(8):
        m = dict(shared)
        m["x"] = np.ascontiguousarray(x[b])
        in_maps.append(m)
    res = run_bass_kernel_spmd(nc, in_maps, core_ids=list(range(8)))
    out = np.stack([np.asarray(r["out"]) for r in res.results], axis=0)
    return out.astype(np.float32, copy=False)
```

```python
import contextlib
import numpy as np
import concourse.bass as bass
import concourse.mybir as mybir
from concourse.bass_utils import run_bass_kernel_spmd

F32 = mybir.dt.float32
BF16 = mybir.dt.bfloat16
AF = mybir.ActivationFunctionType
ALU = mybir.AluOpType

D = 1024
DA = 512
DB = 512
DFF = 2816
DIN = 4608
DEPTH = 2
T = 512
NCH = 8
NFF = 22
SLOT = 4096
NSLOT = 7
NSCR = 10
RMS_EPS = 1e-6
LN_EPS = 1e-5

LS = 136
O_N1G, O_BG, O_LNG, O_WSC, O_N2G, O_WFC, O_BFC = 0, 8, 24, 28, 40, 48, 114
O_FG = 2 * LS
NS = O_FG + 8


class Buf:
    __slots__ = ("name", "w", "rd", "rd_dma", "gen")

    def __init__(self, name):
        self.name = name
        self.w = None
        self.rd = {}
        self.rd_dma = []
        self.gen = 0


class Op:
    __slots__ = ("eng", "fn", "deps", "sig", "need", "dma")


class Prog:
    ENGS = ("pe", "act", "dve", "pool", "sp")

    def __init__(self):
        self.q = {e: [] for e in self.ENGS}
        self.dma_cnt = {}
        self.last_dma = {}
        self.final = []

    def add(self, eng, fn, rd=(), wr=(), dma=None):
        op = Op()
        op.eng = eng
        op.fn = fn
        op.need = False
        op.dma = dma
        op.sig = None
        deps = []
        for b in rd:
            if b.w is not None:
                deps.append(b.w)
        for b in wr:
            if b.w is not None:
                deps.append(b.w)
            deps.extend(b.rd.values())
            deps.extend(b.rd_dma)
        if dma is not None and dma in self.last_dma:
            deps.append(self.last_dma[dma])
        seen = set()
        dl = []
        for d in deps:
            if id(d) in seen:
                continue
            seen.add(id(d))
            if eng == "pe" and d.eng == "pe" and d.dma is None:
                continue
            dl.append(d)
            d.need = True
        op.deps = dl
        for b in wr:
            b.w = op
            b.rd = {}
            b.rd_dma = []
        for b in rd:
            if dma is not None:
                b.rd_dma.append(op)
            else:
                b.rd[eng] = op
        if dma is not None:
            self.dma_cnt[dma] = self.dma_cnt.get(dma, 0) + 16
            op.sig = (dma, self.dma_cnt[dma])
            op.need = True
            self.last_dma[dma] = op
        self.q[eng].append(op)
        return op

    def emit(self, block, sems):
        for e in ("pe", "act", "dve", "pool"):
            cnt = 0
            for op in self.q[e]:
                if op.dma is None and op.need:
                    cnt += 1
                    op.sig = (e, cnt)

        def run(e, eng):
            waited = {}
            for op in self.q[e]:
                for d in op.deps:
                    sname, val = d.sig
                    if waited.get(sname, 0) >= val:
                        continue
                    eng.wait_ge(sems[sname], val)
                    waited[sname] = val
                ins = op.fn(eng)
                if op.dma is not None:
                    ins.then_inc(sems[op.dma], 16)
                elif op.need:
                    ins.then_inc(sems[e], 1)
            for (fe, sname) in self.final:
                if fe == e:
                    eng.wait_ge(sems[sname], self.dma_cnt[sname])

        @block.tensor
        def _(t):
            run("pe", t)

        @block.scalar
        def _(s):
            run("act", s)

        @block.vector
        def _(v):
            run("dve", v)

        @block.gpsimd
        def _(g):
            run("pool", g)

        @block.sync
        def _(sp):
            run("sp", sp)


class Ring:
    def __init__(self, items):
        self.items = items
        self.i = 0

    def get(self):
        b, ap = self.items[self.i % len(self.items)]
        self.i += 1
        return b, ap


def piece_table():
    names = []
    for g in (0, 1, 2, 3, 4):
        names.append(("G%d" % g, 4096))
    for grp in (0, 1):
        names.append(("G%d" % (5 + grp), 4096))
        names.append(("G%d" % (7 + grp), 4096))
        names.append(("PA%d" % grp, 2048))
        names.append(("PB%d" % grp, 2048))
    names.append(("WO0", 4096))
    names.append(("WO1", 4096))
    for i in range(11):
        names.append(("UP%d" % i, 4096))
    for cp in range(4):
        names.append(("DN0_%d" % cp, 2816))
        names.append(("DN1_%d" % cp, 2816))
    tab = []
    off = 0
    for n, sz in names:
        tab.append({"name": n, "size": sz, "off": off})
        off += sz
    return tab, off


PIECES, WTOT = piece_table()
NPIECE = len(PIECES)


def build(S):
    NT = S // T
    nc = bass.Bass("TRN2", target_bir_lowering=False)
    P = Prog()

    x_d = nc.dram_tensor("x", [S, D], F32, kind="ExternalInput")
    o_d = nc.dram_tensor("out", [S, D], F32, kind="ExternalOutput")
    win_d = nc.dram_tensor("w_in", [DEPTH * D, DIN], F32, kind="ExternalInput")
    wbr_d = nc.dram_tensor("w_branch", [DEPTH * 2 * DA, D], F32, kind="ExternalInput")
    wout_d = nc.dram_tensor("w_out", [DEPTH * D, D], F32, kind="ExternalInput")
    wup_d = nc.dram_tensor("w_up", [DEPTH * D, 2 * DFF], F32, kind="ExternalInput")
    wdn_d = nc.dram_tensor("w_down", [DEPTH * DFF, D], F32, kind="ExternalInput")
    sm_d = nc.dram_tensor("smalls", [128, NS], F32, kind="ExternalInput")
    wmt_d = nc.dram_tensor("wmt", [DEPTH * 4 * 128, 128], F32, kind="ExternalInput")
    lnb_d = nc.dram_tensor("lnb", [1, DEPTH * DA], F32, kind="ExternalInput")
    bsr_d = nc.dram_tensor("bsr", [1, DEPTH * DA], F32, kind="ExternalInput")
    idn_d = nc.dram_tensor("ident", [128, 128], F32, kind="ExternalInput")
    ws_d = nc.dram_tensor("wscratch", [DEPTH * 128, WTOT], BF16)

    with contextlib.ExitStack() as es:
        def sb(name, shape, dt):
            return es.enter_context(nc.sbuf_tensor(name, shape, dt))

        xs = sb("xs", [128, NCH, T], F32)
        xin = sb("xin", [128, 4, D], F32)
        hs = sb("hs", [128, NCH, T], BF16)
        sqs = sb("sqs", [128, 4, T], BF16)
        guv = sb("guv", [128, 8, T], F32)
        vns = sb("vns", [128, 4, T], BF16)
        yas = sb("yas", [128, 4, T], BF16)
        ybs = sb("ybs", [128, 4, T], BF16)
        mgs = sb("mgs", [128, NCH, T], BF16)
        acts = sb("acts", [128, NFF, T], BF16)
        pbs = sb("pbs", [128, 2, T + 2], F32)
        gbs = sb("gbs", [128, 3, T + 2], F32)
        scr = sb("scr", [128, NSCR, T], F32)
        ring = sb("ring", [128, NSLOT * SLOT], BF16)
        ident = sb("identsb", [128, 128], F32)
        ones_b = sb("ones_b", [128, 128], BF16)
        ones_f = sb("ones_f", [1, 128], F32)
        wmt_b = sb("wmt_b", [128, DEPTH * 4, 128], BF16)
        cst = sb("cst", [128, DEPTH * 4, 128], F32)
        sm = sb("sm", [128, NS], F32)
        bsr = sb("bsrsb", [1, DEPTH * DA], F32)
        halo_b = sb("halo_b", [128, DEPTH * 4, 2], F32)
        halo_g = sb("halo_g", [128, DEPTH * NFF, 2], F32)
        stats = sb("stats", [128, 4, 6], F32)
        mv = sb("mv", [128, 4, 2], F32)
        sd4 = sb("sd4", [128, 4], F32)
        rs4 = sb("rs4", [128, 4], F32)
        nm4 = sb("nm4", [128, 4], F32)
        dm = sb("dm", [128, 2], F32)
        rstd_sb = sb("rstd_sb", [128, T], F32)
        rtm = sb("rtm", [128, 4], F32)
        psb = [es.enter_context(nc.psum_tensor("ps%d" % i, [128, T], F32)) for i in range(8)]

        NCS, NOS = 8, 4
        sem_names = ["pe", "act", "dve", "pool", "d_in", "d_misc"] + \
                    ["d_w%d" % i for i in range(NSLOT)] + ["d_c%d" % i for i in range(NCS)] + \
                    ["d_o%d" % i for i in range(NOS)]
        cnt = {"cast": 0, "out": 0}
        sems = {n: es.enter_context(nc.semaphore(n)) for n in sem_names}
        block = es.enter_context(nc.Block())

        B_x = [Buf("x%d" % c) for c in range(NCH)]
        B_xin = Buf("xin")
        B_h = [Buf("h%d" % c) for c in range(NCH)]
        B_guv = [Buf("guv%d" % c) for c in range(8)]
        B_vn = [Buf("vn%d" % c) for c in range(4)]
        B_ya = [Buf("ya%d" % c) for c in range(4)]
        B_yb = [Buf("yb%d" % c) for c in range(4)]
        B_mg = [Buf("mg%d" % c) for c in range(NCH)]
        B_act = [Buf("act%d" % c) for c in range(NFF)]
        B_const = Buf("const")
        B_halo_b = [Buf("hb%d" % i) for i in range(DEPTH * 4)]
        B_halo_g = [Buf("hg%d" % i) for i in range(DEPTH * NFF)]
        B_st = Buf("stats")
        B_slot = [Buf("slot%d" % i) for i in range(NSLOT)]
        B_ws = [[[Buf("ws%d_%d_%d" % (l, i, j)) for j in range(2)] for i in range(NPIECE)] for l in range(DEPTH)]

        R_ps = Ring([(Buf("ps%d" % i), psb[i]) for i in range(7)])
        B_ss = Buf("ss")
        ps_ss = psb[7]
        B_dm = Buf("dm")
        B_rstd = Buf("rstd")
        B_rtm = Buf("rtm")
        R_scr = Ring([(Buf("scr%d" % i), scr[:, i, :]) for i in range(NSCR)])
        R_sq = Ring([(Buf("sq%d" % i), sqs[:, i, :]) for i in range(4)])
        R_pb = Ring([((Buf("pb%d" % i), Buf("pbh%d" % i)), pbs[:, i, :]) for i in range(2)])
        R_gb = Ring([((Buf("gb%d" % i), Buf("gbh%d" % i)), gbs[:, i, :]) for i in range(3)])

        def smc(col):
            return sm[:, col:col + 1]

        P.add("sp", lambda e: e.dma_start(out=sm[:], in_=sm_d.ap()[:, :]), wr=[B_const], dma="d_misc")
        P.add("sp", lambda e: e.dma_start(out=ident[:], in_=idn_d.ap()[:, :]), wr=[B_const], dma="d_misc")
        P.add("sp", lambda e: e.dma_start(out=bsr[:], in_=bsr_d.ap()[:, :]), wr=[B_const], dma="d_misc")
        STG = [Buf("stg")] + B_guv[0:4]
        wm_f = guv[:, 0:2, :].rearrange("p c (a i) -> p (c a) i", a=4)
        lnb_bc = guv[:, 2:4, :].rearrange("p c t -> p (c t)")
        P.add("sp", lambda e: e.dma_start(
            out=wm_f, in_=wmt_d.ap().rearrange("(a j) i -> j a i", j=128)), wr=STG, dma="d_misc")
        P.add("sp", lambda e: e.dma_start(
            out=lnb_bc, in_=lnb_d.ap()[0:1, :].broadcast_to([128, DEPTH * DA])), wr=STG, dma="d_misc")
        P.add("pool", lambda e: e.memset(ones_b[:], 1.0), wr=[B_const])
        P.add("pool", lambda e: e.memset(ones_f[:], 1.0), wr=[B_const])
        P.add("pool", lambda e: e.memset(dm[:], 1.0), wr=[B_dm])
        P.add("pool", lambda e: e.memset(halo_b[:], 0.0), wr=B_halo_b)
        P.add("pool", lambda e: e.memset(halo_g[:], 0.0), wr=B_halo_g)
        P.add("dve", lambda e: e.memset(wm_f[64:128, :, 0:64], 0.0), rd=STG, wr=STG)
        P.add("dve", lambda e: e.tensor_copy(out=wmt_b[:], in_=wm_f), rd=STG, wr=[B_const])
        for l in range(DEPTH):
            pb_, pap = R_ps.get()
            for hd in range(4):
                a = l * 4 + hd
                P.add("pe", lambda e, a=a, hd=hd, l=l, pap=pap: e.matmul(
                    pap[:, hd * 128:(hd + 1) * 128],
                    lhsT=lnb_bc[:, l * DA + hd * 128: l * DA + (hd + 1) * 128],
                    rhs=wm_f[:, a, :], start=True, stop=False), rd=STG, wr=[pb_])
                P.add("pe", lambda e, a=a, hd=hd, l=l, pap=pap: e.matmul(
                    pap[:, hd * 128:(hd + 1) * 128],
                    lhsT=ones_f[0:1, :],
                    rhs=bsr[0:1, l * DA + hd * 128: l * DA + (hd + 1) * 128],
                    start=False, stop=True), rd=[B_const], wr=[pb_])
            P.add("act", lambda e, l=l, pap=pap: e.activation(
                out=cst[:, l * 4:(l + 1) * 4, :],
                in_=pap[:].rearrange("p (a i) -> p a i", a=4), func=AF.Copy), rd=[pb_], wr=[B_const])

        def load_x(n):
            t0 = n * T
            src = bass.AP(x_d, t0 * D, [[D, 128], [128 * D, 4], [1, D]])
            P.add("pool", lambda e: e.dma_start(out=xin[:], in_=src), wr=[B_xin], dma="d_in")

        load_x(0)

        def cast_piece(l, pi):
            pc = PIECES[pi]
            n = pc["name"]
            off = pc["off"]
            row0 = l * 128
            dsts_srcs = []
            if n[0] == "G":
                g = int(n[1:])
                src = bass.AP(win_d, l * D * DIN + g * 512, [[DIN, 128], [128 * DIN, 8], [1, 512]])
                dst = bass.AP(ws_d, row0 * WTOT + off, [[WTOT, 128], [512, 8], [1, 512]])
                dsts_srcs.append((dst, src))
            elif n[0] == "P":
                br = 0 if n[1] == "A" else 1
                g = int(n[2:])
                src = bass.AP(wbr_d, (l * 2 + br) * DA * D + g * 512, [[D, 128], [128 * D, 4], [1, 512]])
                dst = bass.AP(ws_d, row0 * WTOT + off, [[WTOT, 128], [512, 4], [1, 512]])
                dsts_srcs.append((dst, src))
            elif n[0] == "W":
                g = int(n[2:])
                src = bass.AP(wout_d, l * D * D + g * 512, [[D, 128], [128 * D, 8], [1, 512]])
                dst = bass.AP(ws_d, row0 * WTOT + off, [[WTOT, 128], [512, 8], [1, 512]])
                dsts_srcs.append((dst, src))
            elif n[0] == "U":
                i = int(n[2:])
                for gv in range(2):
                    src = bass.AP(wup_d, l * D * 2 * DFF + gv * DFF + i * 256,
                                  [[2 * DFF, 128], [128 * 2 * DFF, 8], [1, 256]])
                    dst = bass.AP(ws_d, row0 * WTOT + off + gv * 256, [[WTOT, 128], [512, 8], [1, 256]])
                    dsts_srcs.append((dst, src))
            else:
                kh = int(n[2])
                cp = int(n[4:])
                src = bass.AP(wdn_d, (l * DFF + kh * 11 * 128) * D + cp * 256,
                              [[D, 128], [128 * D, 11], [1, 256]])
                dst = bass.AP(ws_d, row0 * WTOT + off, [[WTOT, 128], [256, 11], [1, 256]])
                dsts_srcs.append((dst, src))
            for j, (dst, src) in enumerate(dsts_srcs):
                P.add("pool", lambda e, dst=dst, src=src: e.dma_start(out=dst, in_=src),
                      wr=[B_ws[l][pi][j]], dma="d_c%d" % (cnt["cast"] % NCS))
                cnt["cast"] += 1

        for l in range(DEPTH):
            for pi in range(NPIECE):
                cast_piece(l, pi)

        NQ = NT * DEPTH * NPIECE
        wst = {"next_load": 0, "next_acq": 0, "next_rel": 0}

        def issue_load():
            q = wst["next_load"]
            if q >= NQ:
                return
            wst["next_load"] += 1
            l = (q // NPIECE) % DEPTH
            pi = q % NPIECE
            pc = PIECES[pi]
            s = q % NSLOT
            src = bass.AP(ws_d, l * 128 * WTOT + pc["off"], [[WTOT, 128], [1, pc["size"]]])
            dst = ring[:, s * SLOT: s * SLOT + pc["size"]]
            P.add("sp", lambda e, dst=dst, src=src: e.dma_start(out=dst, in_=src),
                  rd=B_ws[l][pi], wr=[B_slot[s]], dma="d_w%d" % s)

        def acquire(name):
            q = wst["next_acq"]
            wst["next_acq"] += 1
            pc = PIECES[q % NPIECE]
            assert pc["name"] == name, (pc["name"], name)
            s = q % NSLOT
            return B_slot[s], s * SLOT

        def release(k=1):
            for _ in range(k):
                wst["next_rel"] += 1
                assert wst["next_rel"] <= wst["next_acq"]
                issue_load()

        for _ in range(NSLOT):
            issue_load()

        def mm(out_ap, lhsT, rhs, start, stop, rd, wr):
            P.add("pe", lambda e: e.matmul(out_ap, lhsT=lhsT, rhs=rhs, start=start, stop=stop), rd=rd, wr=wr)

        def act_fn(out_ap, in_ap, func, rd, wr, scale=None, bias=None):
            kw = {}
            if scale is not None:
                kw["scale"] = scale
            if bias is not None:
                kw["bias"] = bias
            P.add("act", lambda e: e.activation(out=out_ap, in_=in_ap, func=func, **kw), rd=rd, wr=wr)

        def stt(out_ap, in0, scalar, in1, op0, op1, rd, wr):
            P.add("dve", lambda e: e.scalar_tensor_tensor(out=out_ap, in0=in0, scalar=scalar, in1=in1,
                                                          op0=op0, op1=op1), rd=rd, wr=wr)

        def tt(out_ap, in0, in1, op, rd, wr, eng="dve"):
            P.add(eng, lambda e: e.tensor_tensor(out=out_ap, in0=in0, in1=in1, op=op), rd=rd, wr=wr)

        nst = {"pend": None, "n": 0}

        def sqrt_preload():
            act_fn(dm[:, 0:1], dm[:, 1:2], AF.Sqrt, rd=[B_dm], wr=[B_dm])

        def norm_flush():
            if nst["pend"] is not None:
                qb, qap = nst["pend"]
                i = nst["n"]
                mm(ps_ss[:], ones_b[:], qap, i == 0, i == NCH - 1, rd=[qb, B_const], wr=[B_ss])
                nst["n"] = i + 1
                nst["pend"] = None

        HP = [(B_vn[k], vns[:, k, :]) for k in range(4)] + [(B_ya[k], yas[:, k, :]) for k in range(4)]

        def norm_sq(c, gcol=None, hp_eng="act"):
            if gcol is not None:
                hb_, hap = HP[c]
                if hp_eng == "act":
                    act_fn(hap, xs[:, c, :], AF.Copy, rd=[B_x[c], B_const], wr=[hb_], scale=smc(gcol + c))
                else:
                    P.add("dve", lambda e: e.tensor_scalar(out=hap, in0=xs[:, c, :], scalar1=smc(gcol + c),
                                                           scalar2=None, op0=ALU.mult),
                          rd=[B_x[c], B_const], wr=[hb_])
            qb, qap = R_sq.get()
            act_fn(qap, xs[:, c, :], AF.Square, rd=[B_x[c]], wr=[qb])
            norm_flush()
            nst["pend"] = (qb, qap)

        def norm_rstd_lazy():
            nst["lazy"] = True

        def norm_rstd():
            nst["lazy"] = False
            norm_flush()
            assert nst["n"] == NCH
            nst["n"] = 0
            sb_, sap = R_scr.get()
            act_fn(sap, ps_ss[:], AF.Sqrt, rd=[B_ss], wr=[sb_], scale=1.0 / D, bias=RMS_EPS)
            P.add("dve", lambda e: e.reciprocal(out=rstd_sb[:], in_=sap), rd=[sb_], wr=[B_rstd])

        def norm_apply(gcol, outs, cs):
            for c in cs:
                ob, oap = outs[c]
                stt(oap, xs[:, c, :], smc(gcol + c), rstd_sb[:], ALU.mult, ALU.mult,
                    rd=[B_x[c], B_rstd, B_const], wr=[ob])

        def fixup(pb_, pap):
            if nst.get("lazy"):
                norm_rstd()
            tt(pap[:], pap[:], rstd_sb[:], ALU.mult, rd=[pb_, B_rstd], wr=[pb_])

        def layer(l):
            so = l * LS
            H = [(B_h[c], hs[:, c, :]) for c in range(NCH)]
            norm_rstd_lazy()
            wb, wo = acquire("G0")
            for c in range(4):
                pb_, pap = R_ps.get()
                for k in range(NCH):
                    mm(pap[:], ring[:, wo + k * 512 + c * 128: wo + k * 512 + (c + 1) * 128], HP[k][1],
                       k == 0, k == NCH - 1, rd=[wb, HP[k][0]], wr=[pb_])
                fixup(pb_, pap)
                act_fn(guv[:, c, :], pap[:], AF.Gelu_apprx_tanh, rd=[pb_], wr=[B_guv[c]])
            release()
            pbt, papt = R_ps.get()
            for b in range(4):
                mm(papt[:, 2 * b:2 * b + 2], rstd_sb[0:1, b * 128:(b + 1) * 128], ones_f[0:1, 0:2], True, True,
                   rd=[B_rstd, B_const], wr=[pbt])
            act_fn(rtm[:], papt[:, 0:8].rearrange("p (b two) -> p b two", two=2)[:, :, 0], AF.Copy,
                   rd=[pbt], wr=[B_rtm])
            norm_apply(so + O_N1G, H, range(NCH))
            wb, wo = acquire("G1")
            for b in range(4):
                pb_, pap = R_ps.get()
                for k in range(NCH):
                    mm(pap[:], HP[k][1][:, b * 128:(b + 1) * 128], ring[:, wo + k * 512: wo + (k + 1) * 512],
                       k == 0, k == NCH - 1, rd=[wb, HP[k][0]], wr=[pb_])
                act_fn(guv[:, 4 + b, :], pap[:], AF.Gelu_apprx_tanh, rd=[pb_, B_rtm], wr=[B_guv[4 + b]],
                       scale=rtm[:, b:b + 1])
                P.add("dve", lambda e, b=b: e.bn_stats(out=stats[:, b, :], in_=guv[:, 4 + b, :]),
                      rd=[B_guv[4 + b]], wr=[B_st])
                P.add("dve", lambda e, b=b: e.bn_aggr(out=mv[:, b, :], in_=stats[:, b, :]),
                      rd=[B_st], wr=[B_st])
            release()
            def ln_finalize():
                act_fn(sd4[:], mv[:, :, 1], AF.Sqrt, rd=[B_st], wr=[B_st], bias=LN_EPS)
                P.add("dve", lambda e: e.reciprocal(out=rs4[:], in_=sd4[:]), rd=[B_st], wr=[B_st])
                stt(nm4[:], mv[:, :, 0], -1.0, rs4[:], ALU.mult, ALU.mult, rd=[B_st], wr=[B_st])
                for b in range(4):
                    act_fn(vns[:, b, :], guv[:, 4 + b, :], AF.Identity, rd=[B_guv[4 + b], B_st], wr=[B_vn[b]],
                           scale=rs4[:, b:b + 1], bias=nm4[:, b:b + 1])

            def spatial_mix():
                for hd in range(4):
                    pb_, pap = R_ps.get()
                    for b in range(4):
                        mm(pap[:, b * 128:(b + 1) * 128], vns[:, b, hd * 128:(hd + 1) * 128],
                           wmt_b[:, l * 4 + hd, :], True, True, rd=[B_vn[b], B_const], wr=[pb_])
                    p3 = pap[:].rearrange("p (b i) -> p b i", b=4)
                    cb = cst[:, l * 4 + hd, :].unsqueeze(1).broadcast_to([128, 4, 128])
                    stt(p3, p3, smc(so + O_LNG + hd), cb, ALU.mult, ALU.add, rd=[pb_, B_const], wr=[pb_])
                    tt(yas[:, hd, :], pap[:], guv[:, hd, :], ALU.mult, rd=[pb_, B_guv[hd]], wr=[B_ya[hd]])
            wbB, woB = acquire("G2")
            wbC, woC = acquire("G3")
            wbH, woH = acquire("G4")
            def conv_chunk(c):
                pbB, papB = R_ps.get()
                pbC, papC = R_ps.get()
                pbH, papH = R_ps.get()
                for (wb_, wo_, pb_, pap_) in ((wbC, woC, pbC, papC), (wbH, woH, pbH, papH), (wbB, woB, pbB, papB)):
                    for k in range(NCH):
                        mm(pap_[:], ring[:, wo_ + k * 512 + c * 128: wo_ + k * 512 + (c + 1) * 128], hs[:, k, :],
                           k == 0, k == NCH - 1, rd=[wb_, B_h[k]], wr=[pb_])
                csb, csap = R_scr.get()
                act_fn(csap, papC[:], AF.Copy, rd=[pbC], wr=[csb])
                (qb, qh), qap = R_pb.get()
                hb = B_halo_b[l * 4 + c]
                P.add("dve", lambda e, qap=qap, c=c: e.tensor_copy(out=qap[:, 0:2], in_=halo_b[:, l * 4 + c, :]),
                      rd=[hb], wr=[qh])
                tt(qap[:, 2:T + 2], csap, papH[:], ALU.mult, rd=[csb, pbH], wr=[qb])
                P.add("act", lambda e, qap=qap, c=c: e.activation(out=halo_b[:, l * 4 + c, :], in_=qap[:, T:T + 2],
                                                                  func=AF.Copy), rd=[qb], wr=[hb])
                ab, aap = R_scr.get()
                act_fn(aap, qap[:, 2:T + 2], AF.Copy, rd=[qb, B_const], wr=[ab], scale=smc(so + O_WSC + 8 + c))
                b1, b1ap = R_scr.get()
                stt(b1ap, qap[:, 1:T + 1], smc(so + O_WSC + 4 + c), aap, ALU.mult, ALU.add,
                    rd=[qb, qh, ab, B_const], wr=[b1])
                c1, c1ap = R_scr.get()
                stt(c1ap, qap[:, 0:T], smc(so + O_WSC + c), b1ap, ALU.mult, ALU.add,
                    rd=[qb, qh, b1, B_const], wr=[c1])
                tt(ybs[:, c, :], c1ap, papB[:], ALU.mult, rd=[c1, pbB], wr=[B_yb[c]])
            conv_chunk(0)
            ln_finalize()
            conv_chunk(1)
            spatial_mix()
            conv_chunk(2)
            conv_chunk(3)
            release(3)
            for grp in range(2):
                wga, oga = acquire("G%d" % (5 + grp))
                wgb, ogb = acquire("G%d" % (7 + grp))
                wpa, opa = acquire("PA%d" % grp)
                wpb, opb = acquire("PB%d" % grp)
                for cc in range(4):
                    c = grp * 4 + cc
                    pga, apga = R_ps.get()
                    pgb, apgb = R_ps.get()
                    ppa, appa = R_ps.get()
                    ppb, appb = R_ps.get()
                    for k in range(NCH):
                        mm(apga[:], ring[:, oga + k * 512 + cc * 128: oga + k * 512 + (cc + 1) * 128], hs[:, k, :],
                           k == 0, k == NCH - 1, rd=[wga, B_h[k]], wr=[pga])
                    for k in range(NCH):
                        mm(apgb[:], ring[:, ogb + k * 512 + cc * 128: ogb + k * 512 + (cc + 1) * 128], hs[:, k, :],
                           k == 0, k == NCH - 1, rd=[wgb, B_h[k]], wr=[pgb])
                    for k in range(4):
                        mm(appa[:], ring[:, opa + k * 512 + cc * 128: opa + k * 512 + (cc + 1) * 128], yas[:, k, :],
                           k == 0, k == 3, rd=[wpa, B_ya[k]], wr=[ppa])
                    for k in range(4):
                        mm(appb[:], ring[:, opb + k * 512 + cc * 128: opb + k * 512 + (cc + 1) * 128], ybs[:, k, :],
                           k == 0, k == 3, rd=[wpb, B_yb[k]], wr=[ppb])
                    sab, saap = R_scr.get()
                    act_fn(saap, apga[:], AF.Sigmoid, rd=[pga, B_const], wr=[sab], bias=smc(so + O_BG + c))
                    sbb, sbap = R_scr.get()
                    act_fn(sbap, apgb[:], AF.Sigmoid, rd=[pgb, B_const], wr=[sbb], bias=smc(so + O_BG + 8 + c))
                    t1b, t1ap = R_scr.get()
                    tt(t1ap, saap, appa[:], ALU.mult, rd=[sab, ppa], wr=[t1b])
                    t2b, t2ap = R_scr.get()
                    tt(t2ap, sbap, appb[:], ALU.mult, rd=[sbb, ppb], wr=[t2b])
                    tt(mgs[:, c, :], t1ap, t2ap, ALU.add, rd=[t1b, t2b], wr=[B_mg[c]])
                release(4)
            wb0, wo0 = acquire("WO0")
            wb1, wo1 = acquire("WO1")
            sqrt_preload()
            for c in range(NCH):
                wb_, wo_ = (wb0, wo0) if c < 4 else (wb1, wo1)
                cc = c % 4
                pb_, pap = R_ps.get()
                for k in range(NCH):
                    mm(pap[:], ring[:, wo_ + k * 512 + cc * 128: wo_ + k * 512 + (cc + 1) * 128], mgs[:, k, :],
                       k == 0, k == NCH - 1, rd=[wb_, B_mg[k]], wr=[pb_])
                tt(xs[:, c, :], xs[:, c, :], pap[:], ALU.add, rd=[B_x[c], pb_], wr=[B_x[c]])
                norm_sq(c, so + O_N2G)
            release(2)
            norm_rstd_lazy()
            st8 = {}

            def s1(c):
                i, j = c // 2, c % 2
                if j == 0:
                    st8["w"] = acquire("UP%d" % i)
                wb_, wo_ = st8["w"]
                pg, apg = R_ps.get()
                pv, apv = R_ps.get()
                src = HP if c < 4 else H
                for k in range(NCH):
                    mm(apg[:], ring[:, wo_ + k * 512 + j * 128: wo_ + k * 512 + (j + 1) * 128], src[k][1],
                       k == 0, k == NCH - 1, rd=[wb_, src[k][0]], wr=[pg])
                if nst.get("lazy"):
                    norm_rstd()
                for k in range(NCH):
                    mm(apv[:], ring[:, wo_ + k * 512 + 256 + j * 128: wo_ + k * 512 + 256 + (j + 1) * 128],
                       src[k][1], k == 0, k == NCH - 1, rd=[wb_, src[k][0]], wr=[pv])
                if j == 1:
                    release()
                if c < 4:
                    fixup(pg, apg)
                    fixup(pv, apv)
                    norm_apply(so + O_N2G, H, (2 * c, 2 * c + 1))
                (gb_, gh_), gap = R_gb.get()
                hg = B_halo_g[l * NFF + c]
                P.add("dve", lambda e: e.tensor_copy(out=gap[:, 0:2], in_=halo_g[:, l * NFF + c, :]),
                      rd=[hg], wr=[gh_])
                act_fn(gap[:, 2:T + 2], apg[:], AF.Copy, rd=[pg], wr=[gb_])
                act_fn(halo_g[:, l * NFF + c, :], apg[:, T - 2:T], AF.Copy, rd=[pg, gh_], wr=[hg])
                ab, aap = R_scr.get()
                act_fn(aap, apg[:], AF.Identity, rd=[pg, B_const], wr=[ab],
                       scale=smc(so + O_WFC + 2 * NFF + c), bias=smc(so + O_BFC + c))
                st8[c] = (gb_, gh_, gap, ab, aap, pv, apv)

            def s2(c):
                gb_, gh_, gap, ab, aap, pv, apv = st8[c]
                b1, b1ap = R_scr.get()
                stt(b1ap, gap[:, 1:T + 1], smc(so + O_WFC + NFF + c), aap, ALU.mult, ALU.add,
                    rd=[gb_, gh_, ab, B_const], wr=[b1])
                c1, c1ap = R_scr.get()
                stt(c1ap, gap[:, 0:T], smc(so + O_WFC + c), b1ap, ALU.mult, ALU.add,
                    rd=[gb_, gh_, b1, B_const], wr=[c1])
                sgb, sgap = R_scr.get()
                act_fn(sgap, c1ap, AF.Silu, rd=[c1], wr=[sgb])
                st8[c] = (sgb, sgap, pv, apv)

            def s3(c):
                sgb, sgap, pv, apv = st8[c]
                tt(acts[:, c, :], sgap, apv[:], ALU.mult, rd=[sgb, pv], wr=[B_act[c]])
                del st8[c]

            for it in range(NFF + 2):
                if it < NFF:
                    s1(it)
                if 0 <= it - 1 < NFF:
                    s2(it - 1)
                if 0 <= it - 2 < NFF:
                    s3(it - 2)
            sqrt_preload()
            for cp in range(4):
                wa, oa = acquire("DN0_%d" % cp)
                wb2, ob2 = acquire("DN1_%d" % cp)
                for j in range(2):
                    c = cp * 2 + j
                    pb_, pap = R_ps.get()
                    for k in range(NFF):
                        kh, kk = k // 11, k % 11
                        wb_, wo_ = (wa, oa) if kh == 0 else (wb2, ob2)
                        mm(pap[:], ring[:, wo_ + kk * 256 + j * 128: wo_ + kk * 256 + (j + 1) * 128], acts[:, k, :],
                           k == 0, k == NFF - 1, rd=[wb_, B_act[k]], wr=[pb_])
                    tt(xs[:, c, :], xs[:, c, :], pap[:], ALU.add, rd=[B_x[c], pb_], wr=[B_x[c]])
                    norm_sq(c, (l + 1) * LS + O_N1G if l + 1 < DEPTH else None)
                release(2)

        for n in range(NT):
            sqrt_preload()
            for c in range(NCH):
                pb_, pap = R_ps.get()
                for blk in range(4):
                    P.add("pe", lambda e, pap=pap, blk=blk, c=c: e.transpose(
                        out=pap[:, blk * 128:(blk + 1) * 128], in_=xin[:, blk, c * 128:(c + 1) * 128],
                        identity=ident[:]), rd=[B_xin, B_const], wr=[pb_])
                P.add("dve", lambda e, pap=pap, c=c: e.tensor_copy(out=xs[:, c, :], in_=pap[:]),
                      rd=[pb_], wr=[B_x[c]])
                norm_sq(c, O_N1G, hp_eng="dve")
            if n + 1 < NT:
                load_x(n + 1)
            for l in range(DEPTH):
                layer(l)
            norm_rstd()
            norm_apply(O_FG, [(B_guv[c], guv[:, c, :]) for c in range(NCH)], range(NCH))
            for blk in range(4):
                for half in range(2):
                    pb_, pap = R_ps.get()
                    for cc in range(4):
                        c = half * 4 + cc
                        P.add("pe", lambda e, pap=pap, blk=blk, c=c, cc=cc: e.transpose(
                            out=pap[:, cc * 128:(cc + 1) * 128], in_=guv[:, c, blk * 128:(blk + 1) * 128],
                            identity=ident[:]), rd=[B_guv[c], B_const], wr=[pb_])
                    ob, oap = R_scr.get()
                    act_fn(oap, pap[:], AF.Copy, rd=[pb_], wr=[ob])
                    dst = bass.AP(o_d, (n * T + blk * 128) * D + half * 512, [[D, 128], [1, 512]])
                    P.add("pool", lambda e, dst=dst, oap=oap: e.dma_start(out=dst, in_=oap),
                          rd=[ob], dma="d_o%d" % (cnt["out"] % NOS))
                    cnt["out"] += 1
        assert wst["next_acq"] == NQ and wst["next_rel"] == NQ, wst
        for i in range(NOS):
            P.final.append(("pool", "d_o%d" % i))
        P.emit(block, sems)
    return nc


def _cols(v):
    return np.ascontiguousarray(v.reshape(-1, 128).T)


def _taps(w):
    k, m = w.shape
    n = m // 128
    return np.ascontiguousarray(w.reshape(k, n, 128).transpose(2, 0, 1).reshape(128, k * n))


def _smalls(inp):
    parts = []
    for l in range(DEPTH):
        parts += [_cols(inp["norm1_g"][l]), _cols(inp["b_gate"][l]), _cols(inp["gmlp_ln_g"][l]),
                  _taps(inp["w_shortconv"][l]), _cols(inp["norm2_g"][l]), _taps(inp["w_ffn_conv"][l]),
                  _cols(inp["b_ffn_conv"][l])]
    parts.append(_cols(inp["final_g"]))
    sm = np.ascontiguousarray(np.concatenate(parts, axis=1).astype(np.float32))
    assert sm.shape == (128, NS), sm.shape
    return sm


_NC_CACHE = {}


def kernel(**inputs):
    inp = {k: np.asarray(v) for k, v in inputs.items()}
    x = inp["x"]
    Bsz, S, _ = x.shape
    assert Bsz == 8 and S % T == 0
    if S not in _NC_CACHE:
        _NC_CACHE[S] = build(S)
    nc = _NC_CACHE[S]
    shared = {
        "w_in": np.ascontiguousarray(inp["w_in"].reshape(DEPTH * D, DIN)),
        "w_branch": np.ascontiguousarray(inp["w_branch"].reshape(DEPTH * 2 * DA, D)),
        "w_out": np.ascontiguousarray(inp["w_out"].reshape(DEPTH * D, D)),
        "w_up": np.ascontiguousarray(inp["w_ffn_up"].reshape(DEPTH * D, 2 * DFF)),
        "w_down": np.ascontiguousarray(inp["w_ffn_down"].reshape(DEPTH * DFF, D)),
        "smalls": _smalls(inp),
        "wmt": np.ascontiguousarray(inp["w_spatial"].transpose(0, 1, 3, 2).reshape(DEPTH * 4 * 128, 128)),
        "lnb": np.ascontiguousarray(inp["gmlp_ln_b"].reshape(1, DEPTH * DA)),
        "bsr": np.ascontiguousarray(inp["b_spatial"].reshape(1, DEPTH * DA)),
        "ident": np.eye(128, dtype=np.float32),
    }
    in_maps = []
    for b in range(8):
        m = dict(shared)
        m["x"] = np.ascontiguousarray(x[b])
        in_maps.append(m)
    res = run_bass_kernel_spmd(nc, in_maps, core_ids=list(range(8)))
    out = np.stack([np.asarray(r["out"]) for r in res.results], axis=0)
    return out.astype(np.float32, copy=False)
```

```python
import contextlib
import numpy as np
import concourse.bass as bass
import concourse.mybir as mybir
from concourse.bass_utils import run_bass_kernel_spmd

F32 = mybir.dt.float32
BF16 = mybir.dt.bfloat16
AF = mybir.ActivationFunctionType
ALU = mybir.AluOpType

D = 1024
DA = 512
DB = 512
DFF = 2816
DIN = 4608
DEPTH = 2
T = 512
NCH = 8
NFF = 22
SLOT = 4096
NSLOT = 7
NSCR = 10
RMS_EPS = 1e-6
LN_EPS = 1e-5

LS = 136
O_N1G, O_BG, O_LNG, O_WSC, O_N2G, O_WFC, O_BFC = 0, 8, 24, 28, 40, 48, 114
O_FG = 2 * LS
NS = O_FG + 8


class Buf:
    __slots__ = ("name", "w", "rd", "rd_dma", "gen")

    def __init__(self, name):
        self.name = name
        self.w = None
        self.rd = {}
        self.rd_dma = []
        self.gen = 0


class Op:
    __slots__ = ("eng", "fn", "deps", "sig", "need", "dma")


class Prog:
    ENGS = ("pe", "act", "dve", "pool", "sp")

    def __init__(self):
        self.q = {e: [] for e in self.ENGS}
        self.dma_cnt = {}
        self.last_dma = {}
        self.final = []

    def add(self, eng, fn, rd=(), wr=(), dma=None):
        op = Op()
        op.eng = eng
        op.fn = fn
        op.need = False
        op.dma = dma
        op.sig = None
        deps = []
        for b in rd:
            if b.w is not None:
                deps.append(b.w)
        for b in wr:
            if b.w is not None:
                deps.append(b.w)
            deps.extend(b.rd.values())
            deps.extend(b.rd_dma)
        if dma is not None and dma in self.last_dma:
            deps.append(self.last_dma[dma])
        seen = set()
        dl = []
        for d in deps:
            if id(d) in seen:
                continue
            seen.add(id(d))
            if eng == "pe" and d.eng == "pe" and d.dma is None:
                continue
            dl.append(d)
            d.need = True
        op.deps = dl
        for b in wr:
            b.w = op
            b.rd = {}
            b.rd_dma = []
        for b in rd:
            if dma is not None:
                b.rd_dma.append(op)
            else:
                b.rd[eng] = op
        if dma is not None:
            self.dma_cnt[dma] = self.dma_cnt.get(dma, 0) + 16
            op.sig = (dma, self.dma_cnt[dma])
            op.need = True
            self.last_dma[dma] = op
        self.q[eng].append(op)
        return op

    def emit(self, block, sems):
        for e in ("pe", "act", "dve", "pool"):
            cnt = 0
            for op in self.q[e]:
                if op.dma is None and op.need:
                    cnt += 1
                    op.sig = (e, cnt)

        def run(e, eng):
            waited = {}
            for op in self.q[e]:
                for d in op.deps:
                    sname, val = d.sig
                    if waited.get(sname, 0) >= val:
                        continue
                    eng.wait_ge(sems[sname], val)
                    waited[sname] = val
                ins = op.fn(eng)
                if op.dma is not None:
                    ins.then_inc(sems[op.dma], 16)
                elif op.need:
                    ins.then_inc(sems[e], 1)
            for (fe, sname) in self.final:
                if fe == e:
                    eng.wait_ge(sems[sname], self.dma_cnt[sname])

        @block.tensor
        def _(t):
            run("pe", t)

        @block.scalar
        def _(s):
            run("act", s)

        @block.vector
        def _(v):
            run("dve", v)

        @block.gpsimd
        def _(g):
            run("pool", g)

        @block.sync
        def _(sp):
            run("sp", sp)


class Ring:
    def __init__(self, items):
        self.items = items
        self.i = 0

    def get(self):
        b, ap = self.items[self.i % len(self.items)]
        self.i += 1
        return b, ap


def piece_table():
    names = []
    for g in (0, 1, 2, 3, 4):
        names.append(("G%d" % g, 4096))
    for grp in (0, 1):
        names.append(("G%d" % (5 + grp), 4096))
        names.append(("G%d" % (7 + grp), 4096))
        names.append(("PA%d" % grp, 2048))
        names.append(("PB%d" % grp, 2048))
    names.append(("WO0", 4096))
    names.append(("WO1", 4096))
    for i in range(11):
        names.append(("UP%d" % i, 4096))
    for cp in range(4):
        names.append(("DN0_%d" % cp, 2816))
        names.append(("DN1_%d" % cp, 2816))
    tab = []
    off = 0
    for n, sz in names:
        tab.append({"name": n, "size": sz, "off": off})
        off += sz
    return tab, off


PIECES, WTOT = piece_table()
NPIECE = len(PIECES)


def build(S):
    NT = S // T
    nc = bass.Bass("TRN2", target_bir_lowering=False)
    P = Prog()

    x_d = nc.dram_tensor("x", [S, D], F32, kind="ExternalInput")
    o_d = nc.dram_tensor("out", [S, D], F32, kind="ExternalOutput")
    win_d = nc.dram_tensor("w_in", [DEPTH * D, DIN], F32, kind="ExternalInput")
    wbr_d = nc.dram_tensor("w_branch", [DEPTH * 2 * DA, D], F32, kind="ExternalInput")
    wout_d = nc.dram_tensor("w_out", [DEPTH * D, D], F32, kind="ExternalInput")
    wup_d = nc.dram_tensor("w_up", [DEPTH * D, 2 * DFF], F32, kind="ExternalInput")
    wdn_d = nc.dram_tensor("w_down", [DEPTH * DFF, D], F32, kind="ExternalInput")
    sm_d = nc.dram_tensor("smalls", [128, NS], F32, kind="ExternalInput")
    wmt_d = nc.dram_tensor("wmt", [DEPTH * 4 * 128, 128], F32, kind="ExternalInput")
    lnb_d = nc.dram_tensor("lnb", [1, DEPTH * DA], F32, kind="ExternalInput")
    bsr_d = nc.dram_tensor("bsr", [1, DEPTH * DA], F32, kind="ExternalInput")
    idn_d = nc.dram_tensor("ident", [128, 128], F32, kind="ExternalInput")
    ws_d = nc.dram_tensor("wscratch", [DEPTH * 128, WTOT], BF16)

    with contextlib.ExitStack() as es:
        def sb(name, shape, dt):
            return es.enter_context(nc.sbuf_tensor(name, shape, dt))

        xs = sb("xs", [128, NCH, T], F32)
        xin = sb("xin", [128, 4, D], F32)
        hs = sb("hs", [128, NCH, T], BF16)
        sqs = sb("sqs", [128, 4, T], BF16)
        guv = sb("guv", [128, 8, T], F32)
        vns = sb("vns", [128, 4, T], BF16)
        yas = sb("yas", [128, 4, T], BF16)
        ybs = sb("ybs", [128, 4, T], BF16)
        mgs = sb("mgs", [128, NCH, T], BF16)
        acts = sb("acts", [128, NFF, T], BF16)
        pbs = sb("pbs", [128, 2, T + 2], F32)
        gbs = sb("gbs", [128, 3, T + 2], F32)
        scr = sb("scr", [128, NSCR, T], F32)
        ring = sb("ring", [128, NSLOT * SLOT], BF16)
        ident = sb("identsb", [128, 128], F32)
        ones_b = sb("ones_b", [128, 128], BF16)
        ones_f = sb("ones_f", [1, 128], F32)
        wmt_b = sb("wmt_b", [128, DEPTH * 4, 128], BF16)
        cst = sb("cst", [128, DEPTH * 4, 128], F32)
        sm = sb("sm", [128, NS], F32)
        bsr = sb("bsrsb", [1, DEPTH * DA], F32)
        halo_b = sb("halo_b", [128, DEPTH * 4, 2], F32)
        halo_g = sb("halo_g", [128, DEPTH * NFF, 2], F32)
        stats = sb("stats", [128, 4, 6], F32)
        mv = sb("mv", [128, 4, 2], F32)
        sd4 = sb("sd4", [128, 4], F32)
        rs4 = sb("rs4", [128, 4], F32)
        nm4 = sb("nm4", [128, 4], F32)
        dm = sb("dm", [128, 2], F32)
        rstd_sb = sb("rstd_sb", [128, T], F32)
        rtm = sb("rtm", [128, 4], F32)
        psb = [es.enter_context(nc.psum_tensor("ps%d" % i, [128, T], F32)) for i in range(8)]

        NCS, NOS = 8, 4
        sem_names = ["pe", "act", "dve", "pool", "d_in", "d_misc"] + \
                    ["d_w%d" % i for i in range(NSLOT)] + ["d_c%d" % i for i in range(NCS)] + \
                    ["d_o%d" % i for i in range(NOS)]
        cnt = {"cast": 0, "out": 0}
        sems = {n: es.enter_context(nc.semaphore(n)) for n in sem_names}
        block = es.enter_context(nc.Block())

        B_x = [Buf("x%d" % c) for c in range(NCH)]
        B_xin = Buf("xin")
        B_h = [Buf("h%d" % c) for c in range(NCH)]
        B_guv = [Buf("guv%d" % c) for c in range(8)]
        B_vn = [Buf("vn%d" % c) for c in range(4)]
        B_ya = [Buf("ya%d" % c) for c in range(4)]
        B_yb = [Buf("yb%d" % c) for c in range(4)]
        B_mg = [Buf("mg%d" % c) for c in range(NCH)]
        B_act = [Buf("act%d" % c) for c in range(NFF)]
        B_const = Buf("const")
        B_halo_b = [Buf("hb%d" % i) for i in range(DEPTH * 4)]
        B_halo_g = [Buf("hg%d" % i) for i in range(DEPTH * NFF)]
        B_st = Buf("stats")
        B_slot = [Buf("slot%d" % i) for i in range(NSLOT)]
        B_ws = [[[Buf("ws%d_%d_%d" % (l, i, j)) for j in range(2)] for i in range(NPIECE)] for l in range(DEPTH)]

        R_ps = Ring([(Buf("ps%d" % i), psb[i]) for i in range(7)])
        B_ss = Buf("ss")
        ps_ss = psb[7]
        B_dm = Buf("dm")
        B_rstd = Buf("rstd")
        B_rtm = Buf("rtm")
        R_scr = Ring([(Buf("scr%d" % i), scr[:, i, :]) for i in range(NSCR)])
        R_sq = Ring([(Buf("sq%d" % i), sqs[:, i, :]) for i in range(4)])
        R_pb = Ring([((Buf("pb%d" % i), Buf("pbh%d" % i)), pbs[:, i, :]) for i in range(2)])
        R_gb = Ring([((Buf("gb%d" % i), Buf("gbh%d" % i)), gbs[:, i, :]) for i in range(3)])

        def smc(col):
            return sm[:, col:col + 1]

        P.add("sp", lambda e: e.dma_start(out=sm[:], in_=sm_d.ap()[:, :]), wr=[B_const], dma="d_misc")
        P.add("sp", lambda e: e.dma_start(out=ident[:], in_=idn_d.ap()[:, :]), wr=[B_const], dma="d_misc")
        P.add("sp", lambda e: e.dma_start(out=bsr[:], in_=bsr_d.ap()[:, :]), wr=[B_const], dma="d_misc")
        STG = [Buf("stg")] + B_guv[0:4]
        wm_f = guv[:, 0:2, :].rearrange("p c (a i) -> p (c a) i", a=4)
        lnb_bc = guv[:, 2:4, :].rearrange("p c t -> p (c t)")
        P.add("sp", lambda e: e.dma_start(
            out=wm_f, in_=wmt_d.ap().rearrange("(a j) i -> j a i", j=128)), wr=STG, dma="d_misc")
        P.add("sp", lambda e: e.dma_start(
            out=lnb_bc, in_=lnb_d.ap()[0:1, :].broadcast_to([128, DEPTH * DA])), wr=STG, dma="d_misc")
        P.add("pool", lambda e: e.memset(ones_b[:], 1.0), wr=[B_const])
        P.add("pool", lambda e: e.memset(ones_f[:], 1.0), wr=[B_const])
        P.add("pool", lambda e: e.memset(dm[:], 1.0), wr=[B_dm])
        P.add("pool", lambda e: e.memset(halo_b[:], 0.0), wr=B_halo_b)
        P.add("pool", lambda e: e.memset(halo_g[:], 0.0), wr=B_halo_g)
        P.add("dve", lambda e: e.memset(wm_f[64:128, :, 0:64], 0.0), rd=STG, wr=STG)
        P.add("dve", lambda e: e.tensor_copy(out=wmt_b[:], in_=wm_f), rd=STG, wr=[B_const])
        for l in range(DEPTH):
            pb_, pap = R_ps.get()
            for hd in range(4):
                a = l * 4 + hd
                P.add("pe", lambda e, a=a, hd=hd, l=l, pap=pap: e.matmul(
                    pap[:, hd * 128:(hd + 1) * 128],
                    lhsT=lnb_bc[:, l * DA + hd * 128: l * DA + (hd + 1) * 128],
                    rhs=wm_f[:, a, :], start=True, stop=False), rd=STG, wr=[pb_])
                P.add("pe", lambda e, a=a, hd=hd, l=l, pap=pap: e.matmul(
                    pap[:, hd * 128:(hd + 1) * 128],
                    lhsT=ones_f[0:1, :],
                    rhs=bsr[0:1, l * DA + hd * 128: l * DA + (hd + 1) * 128],
                    start=False, stop=True), rd=[B_const], wr=[pb_])
            P.add("act", lambda e, l=l, pap=pap: e.activation(
                out=cst[:, l * 4:(l + 1) * 4, :],
                in_=pap[:].rearrange("p (a i) -> p a i", a=4), func=AF.Copy), rd=[pb_], wr=[B_const])

        def load_x(n):
            t0 = n * T
            src = bass.AP(x_d, t0 * D, [[D, 128], [128 * D, 4], [1, D]])
            P.add("pool", lambda e: e.dma_start(out=xin[:], in_=src), wr=[B_xin], dma="d_in")

        load_x(0)

        def cast_piece(l, pi):
            pc = PIECES[pi]
            n = pc["name"]
            off = pc["off"]
            row0 = l * 128
            dsts_srcs = []
            if n[0] == "G":
                g = int(n[1:])
                src = bass.AP(win_d, l * D * DIN + g * 512, [[DIN, 128], [128 * DIN, 8], [1, 512]])
                dst = bass.AP(ws_d, row0 * WTOT + off, [[WTOT, 128], [512, 8], [1, 512]])
                dsts_srcs.append((dst, src))
            elif n[0] == "P":
                br = 0 if n[1] == "A" else 1
                g = int(n[2:])
                src = bass.AP(wbr_d, (l * 2 + br) * DA * D + g * 512, [[D, 128], [128 * D, 4], [1, 512]])
                dst = bass.AP(ws_d, row0 * WTOT + off, [[WTOT, 128], [512, 4], [1, 512]])
                dsts_srcs.append((dst, src))
            elif n[0] == "W":
                g = int(n[2:])
                src = bass.AP(wout_d, l * D * D + g * 512, [[D, 128], [128 * D, 8], [1, 512]])
                dst = bass.AP(ws_d, row0 * WTOT + off, [[WTOT, 128], [512, 8], [1, 512]])
                dsts_srcs.append((dst, src))
            elif n[0] == "U":
                i = int(n[2:])
                for gv in range(2):
                    src = bass.AP(wup_d, l * D * 2 * DFF + gv * DFF + i * 256,
                                  [[2 * DFF, 128], [128 * 2 * DFF, 8], [1, 256]])
                    dst = bass.AP(ws_d, row0 * WTOT + off + gv * 256, [[WTOT, 128], [512, 8], [1, 256]])
                    dsts_srcs.append((dst, src))
            else:
                kh = int(n[2])
                cp = int(n[4:])
                src = bass.AP(wdn_d, (l * DFF + kh * 11 * 128) * D + cp * 256,
                              [[D, 128], [128 * D, 11], [1, 256]])
                dst = bass.AP(ws_d, row0 * WTOT + off, [[WTOT, 128], [256, 11], [1, 256]])
                dsts_srcs.append((dst, src))
            for j, (dst, src) in enumerate(dsts_srcs):
                P.add("pool", lambda e, dst=dst, src=src: e.dma_start(out=dst, in_=src),
                      wr=[B_ws[l][pi][j]], dma="d_c%d" % (cnt["cast"] % NCS))
                cnt["cast"] += 1

        for l in range(DEPTH):
            for pi in range(NPIECE):
                cast_piece(l, pi)

        NQ = NT * DEPTH * NPIECE
        wst = {"next_load": 0, "next_acq": 0, "next_rel": 0}

        def issue_load():
            q = wst["next_load"]
            if q >= NQ:
                return
            wst["next_load"] += 1
            l = (q // NPIECE) % DEPTH
            pi = q % NPIECE
            pc = PIECES[pi]
            s = q % NSLOT
            src = bass.AP(ws_d, l * 128 * WTOT + pc["off"], [[WTOT, 128], [1, pc["size"]]])
            dst = ring[:, s * SLOT: s * SLOT + pc["size"]]
            P.add("sp", lambda e, dst=dst, src=src: e.dma_start(out=dst, in_=src),
                  rd=B_ws[l][pi], wr=[B_slot[s]], dma="d_w%d" % s)

        def acquire(name):
            q = wst["next_acq"]
            wst["next_acq"] += 1
            pc = PIECES[q % NPIECE]
            assert pc["name"] == name, (pc["name"], name)
            s = q % NSLOT
            return B_slot[s], s * SLOT

        def release(k=1):
            for _ in range(k):
                wst["next_rel"] += 1
                assert wst["next_rel"] <= wst["next_acq"]
                issue_load()

        for _ in range(NSLOT):
            issue_load()

        def mm(out_ap, lhsT, rhs, start, stop, rd, wr):
            P.add("pe", lambda e: e.matmul(out_ap, lhsT=lhsT, rhs=rhs, start=start, stop=stop), rd=rd, wr=wr)

        def act_fn(out_ap, in_ap, func, rd, wr, scale=None, bias=None):
            kw = {}
            if scale is not None:
                kw["scale"] = scale
            if bias is not None:
                kw["bias"] = bias
            P.add("act", lambda e: e.activation(out=out_ap, in_=in_ap, func=func, **kw), rd=rd, wr=wr)

        def stt(out_ap, in0, scalar, in1, op0, op1, rd, wr):
            P.add("dve", lambda e: e.scalar_tensor_tensor(out=out_ap, in0=in0, scalar=scalar, in1=in1,
                                                          op0=op0, op1=op1), rd=rd, wr=wr)

        def tt(out_ap, in0, in1, op, rd, wr, eng="dve"):
            P.add(eng, lambda e: e.tensor_tensor(out=out_ap, in0=in0, in1=in1, op=op), rd=rd, wr=wr)

        nst = {"pend": None, "n": 0}

        def sqrt_preload():
            act_fn(dm[:, 0:1], dm[:, 1:2], AF.Sqrt, rd=[B_dm], wr=[B_dm])

        def norm_flush():
            if nst["pend"] is not None:
                qb, qap = nst["pend"]
                i = nst["n"]
                mm(ps_ss[:], ones_b[:], qap, i == 0, i == NCH - 1, rd=[qb, B_const], wr=[B_ss])
                nst["n"] = i + 1
                nst["pend"] = None

        HP = [(B_vn[k], vns[:, k, :]) for k in range(4)] + [(B_ya[k], yas[:, k, :]) for k in range(4)]

        def norm_sq(c, gcol=None, hp_eng="act"):
            if gcol is not None:
                hb_, hap = HP[c]
                if hp_eng == "act":
                    act_fn(hap, xs[:, c, :], AF.Copy, rd=[B_x[c], B_const], wr=[hb_], scale=smc(gcol + c))
                else:
                    P.add("dve", lambda e: e.tensor_scalar(out=hap, in0=xs[:, c, :], scalar1=smc(gcol + c),
                                                           scalar2=None, op0=ALU.mult),
                          rd=[B_x[c], B_const], wr=[hb_])
            qb, qap = R_sq.get()
            act_fn(qap, xs[:, c, :], AF.Square, rd=[B_x[c]], wr=[qb])
            norm_flush()
            nst["pend"] = (qb, qap)

        def norm_rstd_lazy():
            nst["lazy"] = True

        def norm_rstd():
            nst["lazy"] = False
            norm_flush()
            assert nst["n"] == NCH
            nst["n"] = 0
            sb_, sap = R_scr.get()
            act_fn(sap, ps_ss[:], AF.Sqrt, rd=[B_ss], wr=[sb_], scale=1.0 / D, bias=RMS_EPS)
            P.add("dve", lambda e: e.reciprocal(out=rstd_sb[:], in_=sap), rd=[sb_], wr=[B_rstd])

        def norm_apply(gcol, outs, cs):
            for c in cs:
                ob, oap = outs[c]
                stt(oap, xs[:, c, :], smc(gcol + c), rstd_sb[:], ALU.mult, ALU.mult,
                    rd=[B_x[c], B_rstd, B_const], wr=[ob])

        def fixup(pb_, pap):
            if nst.get("lazy"):
                norm_rstd()
            tt(pap[:], pap[:], rstd_sb[:], ALU.mult, rd=[pb_, B_rstd], wr=[pb_])

        def layer(l):
            so = l * LS
            H = [(B_h[c], hs[:, c, :]) for c in range(NCH)]
            norm_rstd_lazy()
            wb, wo = acquire("G0")
            ub = [R_ps.get() for _ in range(2)]
            for k in range(NCH):
                for c in range(2):
                    mm(ub[c][1][:], ring[:, wo + k * 512 + c * 128: wo + k * 512 + (c + 1) * 128], HP[k][1],
                       k == 0, k == NCH - 1, rd=[wb, HP[k][0]], wr=[ub[c][0]])
            for c in range(4):
                if c < 2:
                    pb_, pap = ub[c]
                else:
                    pb_, pap = R_ps.get()
                    for k in range(NCH):
                        mm(pap[:], ring[:, wo + k * 512 + c * 128: wo + k * 512 + (c + 1) * 128], HP[k][1],
                           k == 0, k == NCH - 1, rd=[wb, HP[k][0]], wr=[pb_])
                fixup(pb_, pap)
                act_fn(guv[:, c, :], pap[:], AF.Gelu_apprx_tanh, rd=[pb_], wr=[B_guv[c]])
            release()
            pbt, papt = R_ps.get()
            for b in range(4):
                mm(papt[:, 2 * b:2 * b + 2], rstd_sb[0:1, b * 128:(b + 1) * 128], ones_f[0:1, 0:2], True, True,
                   rd=[B_rstd, B_const], wr=[pbt])
            act_fn(rtm[:], papt[:, 0:8].rearrange("p (b two) -> p b two", two=2)[:, :, 0], AF.Copy,
                   rd=[pbt], wr=[B_rtm])
            norm_apply(so + O_N1G, H, range(NCH))
            wb, wo = acquire("G1")
            for b in range(4):
                pb_, pap = R_ps.get()
                for k in range(NCH):
                    mm(pap[:], HP[k][1][:, b * 128:(b + 1) * 128], ring[:, wo + k * 512: wo + (k + 1) * 512],
                       k == 0, k == NCH - 1, rd=[wb, HP[k][0]], wr=[pb_])
                act_fn(guv[:, 4 + b, :], pap[:], AF.Gelu_apprx_tanh, rd=[pb_, B_rtm], wr=[B_guv[4 + b]],
                       scale=rtm[:, b:b + 1])
                P.add("dve", lambda e, b=b: e.bn_stats(out=stats[:, b, :], in_=guv[:, 4 + b, :]),
                      rd=[B_guv[4 + b]], wr=[B_st])
                P.add("dve", lambda e, b=b: e.bn_aggr(out=mv[:, b, :], in_=stats[:, b, :]),
                      rd=[B_st], wr=[B_st])
            release()
            def ln_finalize():
                act_fn(sd4[:], mv[:, :, 1], AF.Sqrt, rd=[B_st], wr=[B_st], bias=LN_EPS)
                P.add("dve", lambda e: e.reciprocal(out=rs4[:], in_=sd4[:]), rd=[B_st], wr=[B_st])
                stt(nm4[:], mv[:, :, 0], -1.0, rs4[:], ALU.mult, ALU.mult, rd=[B_st], wr=[B_st])
                for b in range(4):
                    act_fn(vns[:, b, :], guv[:, 4 + b, :], AF.Identity, rd=[B_guv[4 + b], B_st], wr=[B_vn[b]],
                           scale=rs4[:, b:b + 1], bias=nm4[:, b:b + 1])

            def spatial_mix():
                for hd in range(4):
                    pb_, pap = R_ps.get()
                    for b in range(4):
                        mm(pap[:, b * 128:(b + 1) * 128], vns[:, b, hd * 128:(hd + 1) * 128],
                           wmt_b[:, l * 4 + hd, :], True, True, rd=[B_vn[b], B_const], wr=[pb_])
                    p3 = pap[:].rearrange("p (b i) -> p b i", b=4)
                    cb = cst[:, l * 4 + hd, :].unsqueeze(1).broadcast_to([128, 4, 128])
                    stt(p3, p3, smc(so + O_LNG + hd), cb, ALU.mult, ALU.add, rd=[pb_, B_const], wr=[pb_])
                    tt(yas[:, hd, :], pap[:], guv[:, hd, :], ALU.mult, rd=[pb_, B_guv[hd]], wr=[B_ya[hd]])
            wbB, woB = acquire("G2")
            wbC, woC = acquire("G3")
            wbH, woH = acquire("G4")
            def conv_chunk(c):
                pbB, papB = R_ps.get()
                pbC, papC = R_ps.get()
                pbH, papH = R_ps.get()
                for (wb_, wo_, pb_, pap_) in ((wbC, woC, pbC, papC), (wbH, woH, pbH, papH), (wbB, woB, pbB, papB)):
                    for k in range(NCH):
                        mm(pap_[:], ring[:, wo_ + k * 512 + c * 128: wo_ + k * 512 + (c + 1) * 128], hs[:, k, :],
                           k == 0, k == NCH - 1, rd=[wb_, B_h[k]], wr=[pb_])
                csb, csap = R_scr.get()
                act_fn(csap, papC[:], AF.Copy, rd=[pbC], wr=[csb])
                (qb, qh), qap = R_pb.get()
                hb = B_halo_b[l * 4 + c]
                P.add("dve", lambda e, qap=qap, c=c: e.tensor_copy(out=qap[:, 0:2], in_=halo_b[:, l * 4 + c, :]),
                      rd=[hb], wr=[qh])
                tt(qap[:, 2:T + 2], csap, papH[:], ALU.mult, rd=[csb, pbH], wr=[qb])
                P.add("act", lambda e, qap=qap, c=c: e.activation(out=halo_b[:, l * 4 + c, :], in_=qap[:, T:T + 2],
                                                                  func=AF.Copy), rd=[qb], wr=[hb])
                ab, aap = R_scr.get()
                act_fn(aap, qap[:, 2:T + 2], AF.Copy, rd=[qb, B_const], wr=[ab], scale=smc(so + O_WSC + 8 + c))
                b1, b1ap = R_scr.get()
                stt(b1ap, qap[:, 1:T + 1], smc(so + O_WSC + 4 + c), aap, ALU.mult, ALU.add,
                    rd=[qb, qh, ab, B_const], wr=[b1])
                c1, c1ap = R_scr.get()
                stt(c1ap, qap[:, 0:T], smc(so + O_WSC + c), b1ap, ALU.mult, ALU.add,
                    rd=[qb, qh, b1, B_const], wr=[c1])
                tt(ybs[:, c, :], c1ap, papB[:], ALU.mult, rd=[c1, pbB], wr=[B_yb[c]])
            conv_chunk(0)
            ln_finalize()
            conv_chunk(1)
            spatial_mix()
            conv_chunk(2)
            conv_chunk(3)
            release(3)
            for grp in range(2):
                wga, oga = acquire("G%d" % (5 + grp))
                wgb, ogb = acquire("G%d" % (7 + grp))
                wpa, opa = acquire("PA%d" % grp)
                wpb, opb = acquire("PB%d" % grp)
                for cc in range(4):
                    c = grp * 4 + cc
                    pga, apga = R_ps.get()
                    pgb, apgb = R_ps.get()
                    ppa, appa = R_ps.get()
                    ppb, appb = R_ps.get()
                    for k in range(NCH):
                        mm(apga[:], ring[:, oga + k * 512 + cc * 128: oga + k * 512 + (cc + 1) * 128], hs[:, k, :],
                           k == 0, k == NCH - 1, rd=[wga, B_h[k]], wr=[pga])
                    for k in range(NCH):
                        mm(apgb[:], ring[:, ogb + k * 512 + cc * 128: ogb + k * 512 + (cc + 1) * 128], hs[:, k, :],
                           k == 0, k == NCH - 1, rd=[wgb, B_h[k]], wr=[pgb])
                    for k in range(4):
                        mm(appa[:], ring[:, opa + k * 512 + cc * 128: opa + k * 512 + (cc + 1) * 128], yas[:, k, :],
                           k == 0, k == 3, rd=[wpa, B_ya[k]], wr=[ppa])
                    for k in range(4):
                        mm(appb[:], ring[:, opb + k * 512 + cc * 128: opb + k * 512 + (cc + 1) * 128], ybs[:, k, :],
                           k == 0, k == 3, rd=[wpb, B_yb[k]], wr=[ppb])
                    sab, saap = R_scr.get()
                    act_fn(saap, apga[:], AF.Sigmoid, rd=[pga, B_const], wr=[sab], bias=smc(so + O_BG + c))
                    sbb, sbap = R_scr.get()
                    act_fn(sbap, apgb[:], AF.Sigmoid, rd=[pgb, B_const], wr=[sbb], bias=smc(so + O_BG + 8 + c))
                    t1b, t1ap = R_scr.get()
                    tt(t1ap, saap, appa[:], ALU.mult, rd=[sab, ppa], wr=[t1b])
                    t2b, t2ap = R_scr.get()
                    tt(t2ap, sbap, appb[:], ALU.mult, rd=[sbb, ppb], wr=[t2b])
                    tt(mgs[:, c, :], t1ap, t2ap, ALU.add, rd=[t1b, t2b], wr=[B_mg[c]])
                release(4)
            wb0, wo0 = acquire("WO0")
            wb1, wo1 = acquire("WO1")
            sqrt_preload()
            for c in range(NCH):
                wb_, wo_ = (wb0, wo0) if c < 4 else (wb1, wo1)
                cc = c % 4
                pb_, pap = R_ps.get()
                for k in range(NCH):
                    mm(pap[:], ring[:, wo_ + k * 512 + cc * 128: wo_ + k * 512 + (cc + 1) * 128], mgs[:, k, :],
                       k == 0, k == NCH - 1, rd=[wb_, B_mg[k]], wr=[pb_])
                tt(xs[:, c, :], xs[:, c, :], pap[:], ALU.add, rd=[B_x[c], pb_], wr=[B_x[c]])
                norm_sq(c, so + O_N2G)
            release(2)
            norm_rstd_lazy()
            st8 = {}

            def s1(c):
                i, j = c // 2, c % 2
                if j == 0:
                    st8["w"] = acquire("UP%d" % i)
                wb_, wo_ = st8["w"]
                pg, apg = R_ps.get()
                pv, apv = R_ps.get()
                src = HP if c < 4 else H
                if c == 0:
                    for k in range(NCH):
                        mm(apg[:], ring[:, wo_ + k * 512 + j * 128: wo_ + k * 512 + (j + 1) * 128], src[k][1],
                           k == 0, k == NCH - 1, rd=[wb_, src[k][0]], wr=[pg])
                        mm(apv[:], ring[:, wo_ + k * 512 + 256 + j * 128: wo_ + k * 512 + 256 + (j + 1) * 128],
                           src[k][1], k == 0, k == NCH - 1, rd=[wb_, src[k][0]], wr=[pv])
                    if nst.get("lazy"):
                        norm_rstd()
                else:
                    for k in range(NCH):
                        mm(apg[:], ring[:, wo_ + k * 512 + j * 128: wo_ + k * 512 + (j + 1) * 128], src[k][1],
                           k == 0, k == NCH - 1, rd=[wb_, src[k][0]], wr=[pg])
                    for k in range(NCH):
                        mm(apv[:], ring[:, wo_ + k * 512 + 256 + j * 128: wo_ + k * 512 + 256 + (j + 1) * 128],
                           src[k][1], k == 0, k == NCH - 1, rd=[wb_, src[k][0]], wr=[pv])
                if j == 1:
                    release()
                if c < 4:
                    fixup(pg, apg)
                    fixup(pv, apv)
                    norm_apply(so + O_N2G, H, (2 * c, 2 * c + 1))
                (gb_, gh_), gap = R_gb.get()
                hg = B_halo_g[l * NFF + c]
                P.add("dve", lambda e: e.tensor_copy(out=gap[:, 0:2], in_=halo_g[:, l * NFF + c, :]),
                      rd=[hg], wr=[gh_])
                act_fn(gap[:, 2:T + 2], apg[:], AF.Copy, rd=[pg], wr=[gb_])
                act_fn(halo_g[:, l * NFF + c, :], apg[:, T - 2:T], AF.Copy, rd=[pg, gh_], wr=[hg])
                ab, aap = R_scr.get()
                act_fn(aap, apg[:], AF.Identity, rd=[pg, B_const], wr=[ab],
                       scale=smc(so + O_WFC + 2 * NFF + c), bias=smc(so + O_BFC + c))
                st8[c] = (gb_, gh_, gap, ab, aap, pv, apv)

            def s2(c):
                gb_, gh_, gap, ab, aap, pv, apv = st8[c]
                b1, b1ap = R_scr.get()
                stt(b1ap, gap[:, 1:T + 1], smc(so + O_WFC + NFF + c), aap, ALU.mult, ALU.add,
                    rd=[gb_, gh_, ab, B_const], wr=[b1])
                c1, c1ap = R_scr.get()
                stt(c1ap, gap[:, 0:T], smc(so + O_WFC + c), b1ap, ALU.mult, ALU.add,
                    rd=[gb_, gh_, b1, B_const], wr=[c1])
                sgb, sgap = R_scr.get()
                act_fn(sgap, c1ap, AF.Silu, rd=[c1], wr=[sgb])
                st8[c] = (sgb, sgap, pv, apv)

            def s3(c):
                sgb, sgap, pv, apv = st8[c]
                tt(acts[:, c, :], sgap, apv[:], ALU.mult, rd=[sgb, pv], wr=[B_act[c]])
                del st8[c]

            for it in range(NFF + 2):
                if it < NFF:
                    s1(it)
                if 0 <= it - 1 < NFF:
                    s2(it - 1)
                if 0 <= it - 2 < NFF:
                    s3(it - 2)
            sqrt_preload()
            for cp in range(4):
                wa, oa = acquire("DN0_%d" % cp)
                wb2, ob2 = acquire("DN1_%d" % cp)
                for j in range(2):
                    c = cp * 2 + j
                    pb_, pap = R_ps.get()
                    for k in range(NFF):
                        kh, kk = k // 11, k % 11
                        wb_, wo_ = (wa, oa) if kh == 0 else (wb2, ob2)
                        mm(pap[:], ring[:, wo_ + kk * 256 + j * 128: wo_ + kk * 256 + (j + 1) * 128], acts[:, k, :],
                           k == 0, k == NFF - 1, rd=[wb_, B_act[k]], wr=[pb_])
                    tt(xs[:, c, :], xs[:, c, :], pap[:], ALU.add, rd=[B_x[c], pb_], wr=[B_x[c]])
                    norm_sq(c, (l + 1) * LS + O_N1G if l + 1 < DEPTH else None)
                release(2)

        for n in range(NT):
            sqrt_preload()
            for c in range(NCH):
                pb_, pap = R_ps.get()
                for blk in range(4):
                    P.add("pe", lambda e, pap=pap, blk=blk, c=c: e.transpose(
                        out=pap[:, blk * 128:(blk + 1) * 128], in_=xin[:, blk, c * 128:(c + 1) * 128],
                        identity=ident[:]), rd=[B_xin, B_const], wr=[pb_])
                P.add("dve", lambda e, pap=pap, c=c: e.tensor_copy(out=xs[:, c, :], in_=pap[:]),
                      rd=[pb_], wr=[B_x[c]])
                norm_sq(c, O_N1G, hp_eng="dve")
            if n + 1 < NT:
                load_x(n + 1)
            for l in range(DEPTH):
                layer(l)
            norm_rstd()
            norm_apply(O_FG, [(B_guv[c], guv[:, c, :]) for c in range(NCH)], range(NCH))
            for blk in range(4):
                for half in range(2):
                    pb_, pap = R_ps.get()
                    for cc in range(4):
                        c = half * 4 + cc
                        P.add("pe", lambda e, pap=pap, blk=blk, c=c, cc=cc: e.transpose(
                            out=pap[:, cc * 128:(cc + 1) * 128], in_=guv[:, c, blk * 128:(blk + 1) * 128],
                            identity=ident[:]), rd=[B_guv[c], B_const], wr=[pb_])
                    ob, oap = R_scr.get()
                    act_fn(oap, pap[:], AF.Copy, rd=[pb_], wr=[ob])
                    dst = bass.AP(o_d, (n * T + blk * 128) * D + half * 512, [[D, 128], [1, 512]])
                    P.add("pool", lambda e, dst=dst, oap=oap: e.dma_start(out=dst, in_=oap),
                          rd=[ob], dma="d_o%d" % (cnt["out"] % NOS))
                    cnt["out"] += 1
        assert wst["next_acq"] == NQ and wst["next_rel"] == NQ, wst
        for i in range(NOS):
            P.final.append(("pool", "d_o%d" % i))
        P.emit(block, sems)
    return nc


def _cols(v):
    return np.ascontiguousarray(v.reshape(-1, 128).T)


def _taps(w):
    k, m = w.shape
    n = m // 128
    return np.ascontiguousarray(w.reshape(k, n, 128).transpose(2, 0, 1).reshape(128, k * n))


def _smalls(inp):
    parts = []
    for l in range(DEPTH):
        parts += [_cols(inp["norm1_g"][l]), _cols(inp["b_gate"][l]), _cols(inp["gmlp_ln_g"][l]),
                  _taps(inp["w_shortconv"][l]), _cols(inp["norm2_g"][l]), _taps(inp["w_ffn_conv"][l]),
                  _cols(inp["b_ffn_conv"][l])]
    parts.append(_cols(inp["final_g"]))
    sm = np.ascontiguousarray(np.concatenate(parts, axis=1).astype(np.float32))
    assert sm.shape == (128, NS), sm.shape
    return sm


_NC_CACHE = {}


def kernel(**inputs):
    inp = {k: np.asarray(v) for k, v in inputs.items()}
    x = inp["x"]
    Bsz, S, _ = x.shape
    assert Bsz == 8 and S % T == 0
    if S not in _NC_CACHE:
        _NC_CACHE[S] = build(S)
    nc = _NC_CACHE[S]
    shared = {
        "w_in": np.ascontiguousarray(inp["w_in"].reshape(DEPTH * D, DIN)),
        "w_branch": np.ascontiguousarray(inp["w_branch"].reshape(DEPTH * 2 * DA, D)),
        "w_out": np.ascontiguousarray(inp["w_out"].reshape(DEPTH * D, D)),
        "w_up": np.ascontiguousarray(inp["w_ffn_up"].reshape(DEPTH * D, 2 * DFF)),
        "w_down": np.ascontiguousarray(inp["w_ffn_down"].reshape(DEPTH * DFF, D)),
        "smalls": _smalls(inp),
        "wmt": np.ascontiguousarray(inp["w_spatial"].transpose(0, 1, 3, 2).reshape(DEPTH * 4 * 128, 128)),
        "lnb": np.ascontiguousarray(inp["gmlp_ln_b"].reshape(1, DEPTH * DA)),
        "bsr": np.ascontiguousarray(inp["b_spatial"].reshape(1, DEPTH * DA)),
        "ident": np.eye(128, dtype=np.float32),
    }
    in_maps = []
    for b in range(8):
        m = dict(shared)
        m["x"] = np.ascontiguousarray(x[b])
        in_maps.append(m)
    res = run_bass_kernel_spmd(nc, in_maps, core_ids=list(range(8)))
    out = np.stack([np.asarray(r["out"]) for r in res.results], axis=0)
    return out.astype(np.float32, copy=False)
```
